# Optimizing a Trainium2 kernel written in Bass

```python
import jax, jax.numpy as jnp
from jax import lax
import numpy as np

D_MODEL = 1024
BATCH = 4
SEQ = 8192
DEPTH = 1

CHUNK = 64
MIX_WIDTH = D_MODEL
CONV_WIDTH = MIX_WIDTH // 2
CONV_GROUPS = 8
CONV_K = 3
GMLP_WIDTH = MIX_WIDTH - CONV_WIDTH
GMLP_HEADS = 8
GMLP_HEAD_DIM = GMLP_WIDTH // GMLP_HEADS
GMLP_BLOCK = 128
D_FF = 4 * D_MODEL
IN_PROJ = 3 * CONV_WIDTH + 2 * GMLP_WIDTH
EPS = 1e-6

kernel_name = "hybrid_conv_gmlp_parallel_block"


def rmsnorm(x, g):
    xf = x.astype(jnp.float32)
    y = xf * lax.rsqrt(jnp.mean(xf * xf, axis=-1, keepdims=True) + EPS)
    return (y * g.astype(jnp.float32)).astype(x.dtype)


def layernorm(x, g, b):
    xf = x.astype(jnp.float32)
    mu = jnp.mean(xf, axis=-1, keepdims=True)
    xc = xf - mu
    y = xc * lax.rsqrt(jnp.mean(xc * xc, axis=-1, keepdims=True) + EPS)
    return (y * g.astype(jnp.float32) + b.astype(jnp.float32)).astype(x.dtype)


def causal_depthwise_conv(z, w):
    s = z.shape[1]
    zp = jnp.pad(z, ((0, 0), (CONV_K - 1, 0), (0, 0)))
    y = zp[:, 0:s] * w[:, 0]
    for k in range(1, CONV_K):
        y = y + zp[:, k:k + s] * w[:, k]
    return y


def spatial_gating(v, w_s, b_s):
    bsz, s, _ = v.shape
    nb = s // GMLP_BLOCK
    idx = jnp.arange(GMLP_BLOCK)
    mask = (idx[None, :] // CHUNK) <= (idx[:, None] // CHUNK)
    w = jnp.where(mask[None], w_s, jnp.zeros_like(w_s))
    vb = v.reshape(bsz, nb, GMLP_BLOCK, GMLP_HEADS, GMLP_HEAD_DIM)
    out = jnp.einsum('hij,bnjhd->bnihd', w, vb)
    out = out + jnp.transpose(b_s)[None, None, :, :, None]
    return out.reshape(bsz, s, GMLP_WIDTH)


def setup_inputs(seed: int = 0) -> dict:
    key = jax.random.key(seed)
    ks = jax.random.split(key, 16)
    f32 = jnp.float32
    nrm = lambda k, shape, scale: (jax.random.normal(k, shape, f32) * scale)
    return {
        "x": jax.random.normal(ks[0], (BATCH, SEQ, D_MODEL), f32),
        "norm1_g": 1.0 + nrm(ks[1], (DEPTH, D_MODEL), 0.02),
        "w_in": nrm(ks[2], (DEPTH, D_MODEL, IN_PROJ), D_MODEL ** -0.5),
        "conv_w": nrm(ks[3], (DEPTH, CONV_WIDTH, CONV_K), CONV_K ** -0.5),
        "gmlp_ln_g": 1.0 + nrm(ks[4], (DEPTH, GMLP_WIDTH), 0.02),
        "gmlp_ln_b": nrm(ks[5], (DEPTH, GMLP_WIDTH), 0.02),
        "gmlp_ws": nrm(ks[6], (DEPTH, GMLP_HEADS, GMLP_BLOCK, GMLP_BLOCK), GMLP_BLOCK ** -0.5),
        "gmlp_bs": 1.0 + nrm(ks[7], (DEPTH, GMLP_HEADS, GMLP_BLOCK), 0.02),
        "out_norm_conv_g": 1.0 + nrm(ks[8], (DEPTH, CONV_WIDTH), 0.02),
        "out_norm_gmlp_g": 1.0 + nrm(ks[9], (DEPTH, GMLP_WIDTH), 0.02),
        "w_out": nrm(ks[10], (DEPTH, MIX_WIDTH, D_MODEL), MIX_WIDTH ** -0.5),
        "norm2_g": 1.0 + nrm(ks[11], (DEPTH, D_MODEL), 0.02),
        "w_up": nrm(ks[12], (DEPTH, D_MODEL, D_FF), D_MODEL ** -0.5),
        "w_down": nrm(ks[13], (DEPTH, D_FF, D_MODEL), D_FF ** -0.5),
        "final_g": 1.0 + nrm(ks[14], (D_MODEL,), 0.02),
    }


def reference(x, norm1_g, w_in, conv_w, gmlp_ln_g, gmlp_ln_b, gmlp_ws, gmlp_bs,
              out_norm_conv_g, out_norm_gmlp_g, w_out, norm2_g, w_up, w_down, final_g):
    c = CONV_WIDTH
    for l in range(DEPTH):
        h = rmsnorm(x, norm1_g[l])
        p = jnp.einsum('bsd,de->bse', h, w_in[l])
        b_gate = p[..., 0:c]
        c_gate = p[..., c:2 * c]
        xc = p[..., 2 * c:3 * c]
        u = p[..., 3 * c:3 * c + GMLP_WIDTH]
        v = p[..., 3 * c + GMLP_WIDTH:]
        y_a = b_gate * causal_depthwise_conv(c_gate * xc, conv_w[l])
        v = layernorm(v, gmlp_ln_g[l], gmlp_ln_b[l])
        y_b = u * spatial_gating(v, gmlp_ws[l], gmlp_bs[l])
        y = jnp.concatenate([rmsnorm(y_a, out_norm_conv_g[l]),
                             rmsnorm(y_b, out_norm_gmlp_g[l])], axis=-1)
        x = x + jnp.einsum('bse,ed->bsd', y, w_out[l])
        h2 = rmsnorm(x, norm2_g[l])
        a = jax.nn.relu(jnp.einsum('bsd,df->bsf', h2, w_up[l]))
        x = x + jnp.einsum('bsf,fd->bsd', a * a, w_down[l])
    return rmsnorm(x, final_g)
```

```python
import contextlib
import numpy as np
import concourse.bass as bass
import concourse.mybir as mybir
from concourse.bass_utils import run_bass_kernel_spmd

F32 = mybir.dt.float32
BF16 = mybir.dt.bfloat16
ALU = mybir.AluOpType
AF = mybir.ActivationFunctionType

NCORES = 8
D = 1024
SEQ = 8192
BATCH = 4
TOK = SEQ * BATCH // NCORES
MT = 512
NMT = TOK // MT
NGRP = NMT // 2
EPS = 1e-6
NPRM = 52
HW = 32

ENGS = ("pe", "act", "dve", "pool", "sp")


class Op:
    __slots__ = ("eng", "fn", "dma_key", "deps", "needs_inc", "sem", "semval", "waits",
                 "idx", "dur", "preds", "succs", "npred", "ready_t", "start", "finish", "bl", "prio")

    def __init__(self, eng, fn, dma_key, idx, dur):
        self.eng = eng
        self.fn = fn
        self.dma_key = dma_key
        self.idx = idx
        self.dur = dur
        self.deps = []
        self.needs_inc = False
        self.sem = None
        self.semval = None
        self.waits = []


class Sched:
    def __init__(self):
        self.ops = []
        self.by_eng = {e: [] for e in ENGS}
        self.last_w = {}
        self.readers = {}
        self.two_pass = False
        self.event_order = False

    def add(self, eng, fn, reads=(), writes=(), dma_key=None, dur=None, n=512, nbytes=0, prio=1):
        if dur is None:
            if dma_key is not None:
                dur = 4.0 + nbytes / 80e3
            elif eng == "act":
                dur = (n + 335) / 1200.0
            elif eng == "dve":
                dur = (n + 151) / 960.0
            elif eng == "pool":
                dur = (2.2 * n + 160) / 1000.0
            else:
                dur = 0.25
        op = Op(eng, fn, dma_key, len(self.ops), dur)
        op.prio = prio
        for k in reads:
            w = self.last_w.get(k)
            if w is not None:
                op.deps.append((w, "RAW"))
        for k in writes:
            w = self.last_w.get(k)
            if w is not None:
                op.deps.append((w, "WAW"))
            for r in self.readers.get(k, ()):
                op.deps.append((r, "WAR"))
        for k in reads:
            self.readers.setdefault(k, []).append(op)
        for k in writes:
            self.last_w[k] = op
            self.readers[k] = []
        self.ops.append(op)
        self.by_eng[eng].append(op)
        return op

    def schedule(self, lat=0.2):
        ops = self.ops
        for op in ops:
            preds = {}
            for d, _kind in op.deps:
                if d is not op:
                    preds[id(d)] = d
            op.preds = list(preds.values())
            op.succs = []
        for op in ops:
            for d in op.preds:
                d.succs.append(op)
        for op in reversed(ops):
            b = 0.0
            for sc in op.succs:
                if sc.bl + lat > b:
                    b = sc.bl + lat
            op.bl = b + op.dur

        def run(keyfn, window):
            for op in ops:
                op.npred = len(op.preds)
                op.ready_t = 0.0
            ready = {e: [] for e in ENGS}
            for op in ops:
                if op.npred == 0:
                    ready[op.eng].append(op)
            t_free = {e: 0.0 for e in ENGS}
            order = []
            while True:
                best = None
                for e in ENGS:
                    lst = ready[e]
                    if not lst:
                        continue
                    tf = t_free[e]
                    horizon = max(tf, min(o.ready_t for o in lst)) + window
                    cand = None
                    for op in lst:
                        if op.ready_t <= horizon:
                            k = keyfn(op)
                            if cand is None or k < cand[0]:
                                cand = (k, op)
                    op = cand[1]
                    st = op.ready_t if op.ready_t > tf else tf
                    if best is None or (st, op.idx) < best[0]:
                        best = ((st, op.idx), op)
                if best is None:
                    break
                (st, _), op = best
                ready[op.eng].remove(op)
                op.start = st
                op.finish = st + op.dur
                t_free[op.eng] = st + (0.1 if op.dma_key is not None else op.dur)
                order.append(op)
                for sc in op.succs:
                    sc.npred -= 1
                    rt = op.finish + lat
                    if rt > sc.ready_t:
                        sc.ready_t = rt
                    if sc.npred == 0:
                        ready[sc.eng].append(sc)
            assert len(order) == len(ops)
            return order

        order = run(lambda op: (op.prio, -op.bl, op.idx), 0.0)
        if self.two_pass:
            total = max(o.finish for o in order)
            ls = {}
            for op in reversed(order):
                v = total
                for sc in op.succs:
                    c = ls[id(sc)] - lat
                    if c < v:
                        v = c
                v -= op.dur
                if op.eng == "pe" and op.start < v:
                    v = op.start
                ls[id(op)] = v
            order = run(lambda op: (ls[id(op)], op.idx), 0.4)
        if self.event_order:
            ev = {}
            for op in order:
                v = 0.0
                for p in op.preds:
                    c = p.finish if (p.eng == "pe" or p.dma_key is not None) else ev[id(p)]
                    if c > v:
                        v = c
                ev[id(op)] = v
            key = {}
            for op in order:
                if op.eng == "pe":
                    key[id(op)] = (op.start, op.start, op.idx)
                else:
                    key[id(op)] = (ev[id(op)], op.start, op.idx)
            order = sorted(order, key=lambda o: key[id(o)])
            by_eng = {e: [o for o in order if o.eng == e] for e in ENGS}
            pos = {e: 0 for e in ENGS}
            t_free = {e: 0.0 for e in ENGS}
            done = set()
            n_done = 0
            while n_done < len(order):
                progressed = False
                for e in ENGS:
                    while pos[e] < len(by_eng[e]):
                        op = by_eng[e][pos[e]]
                        if any(id(p) not in done for p in op.preds):
                            break
                        st = t_free[e]
                        for p in op.preds:
                            if p.finish + lat > st:
                                st = p.finish + lat
                        op.start = st
                        op.finish = st + op.dur
                        t_free[e] = st + (0.1 if op.dma_key is not None else op.dur)
                        done.add(id(op))
                        pos[e] += 1
                        n_done += 1
                        progressed = True
                assert progressed, "engine queues deadlock"
        self.ops = order
        self.by_eng = {e: [o for o in order if o.eng == e] for e in ENGS}
        self.est_total = max(o.finish for o in order)

    def resolve(self):
        for op in self.ops:
            need = {}
            for d, kind in op.deps:
                if d is op:
                    continue
                if d.dma_key is None and d.eng == op.eng and op.eng == "pe":
                    continue
                need[id(d)] = d
            op.waits = list(need.values())
            for d in op.waits:
                d.needs_inc = True
        dma_keys = []
        for op in self.ops:
            if op.dma_key is not None:
                op.needs_inc = True
                if op.dma_key not in dma_keys:
                    dma_keys.append(op.dma_key)
        return dma_keys

    def assign(self, eng_sems, dma_sems):
        cnt = {e: 0 for e in ENGS}
        dcnt = {}
        for op in self.ops:
            if op.dma_key is not None:
                dcnt[op.dma_key] = dcnt.get(op.dma_key, 0) + 16
                op.sem = dma_sems[op.dma_key]
                op.semval = dcnt[op.dma_key]
            elif op.needs_inc:
                cnt[op.eng] += 1
                op.sem = eng_sems[op.eng]
                op.semval = cnt[op.eng]
        self.final_dma = dict(dcnt)

    def emit(self, eng, e):
        waited = {}
        for op in self.by_eng[eng]:
            best = {}
            for d in op.waits:
                key = id(d.sem)
                if d.semval > best.get(key, (None, 0))[1]:
                    best[key] = (d.sem, d.semval)
            for key, (sem, val) in best.items():
                if waited.get(key, 0) >= val:
                    continue
                e.wait_ge(sem, val)
                waited[key] = val
            ins = op.fn(e)
            if op.dma_key is not None:
                ins.then_inc(op.sem, 16)
            elif op.needs_inc:
                ins.then_inc(op.sem, 1)


def build_program(debug=False):
    nc = bass.Bass("TRN2", target_bir_lowering=False)
    S = Sched()
    if debug:
        dbg_z = nc.dram_tensor("dbg_z", [128, 4, 2], F32, kind="ExternalOutput").ap()
        dbg_ch = nc.dram_tensor("dbg_ch", [128, 4, HW], F32, kind="ExternalOutput").ap()
        dbg_rsh = nc.dram_tensor("dbg_rsh", [128, HW], F32, kind="ExternalOutput").ap()
        dbg_hh = nc.dram_tensor("dbg_hh", [128, 8, HW], BF16, kind="ExternalOutput").ap()

    xT = nc.dram_tensor("xT", [NMT, 128, 8 * MT], F32, kind="ExternalInput").ap()
    xh = nc.dram_tensor("xh", [128, 8, HW], F32, kind="ExternalInput").ap()
    prm_d = nc.dram_tensor("prm", [128, NPRM], F32, kind="ExternalInput").ap()
    bsf_d = nc.dram_tensor("bsf", [128, 4, 128], F32, kind="ExternalInput").ap()
    wst_d = nc.dram_tensor("wst", [128, 8, 128], F32, kind="ExternalInput").ap()
    w_in = nc.dram_tensor("w_in", [D, 2560], F32, kind="ExternalInput").ap()
    w_out = nc.dram_tensor("w_out", [D, D], F32, kind="ExternalInput").ap()
    w_up = nc.dram_tensor("w_up", [D, 4096], F32, kind="ExternalInput").ap()
    w_down = nc.dram_tensor("w_down", [4096, D], F32, kind="ExternalInput").ap()
    oT = nc.dram_tensor("oT", [NMT, 128, 8 * MT], F32, kind="ExternalOutput").ap()

    def scr(name, n):
        return nc.dram_tensor(name, [128, 8, n], BF16, kind="Internal").ap()

    pieces = []
    pieces.append(("V", scr("sc_v", 512), w_in[:, 2048:2560].rearrange("(k p) n -> p k n", p=128), 512))
    pieces.append(("BC", scr("sc_bc", 1024), w_in[:, 0:1024].rearrange("(k p) n -> p k n", p=128), 1024))
    pieces.append(("XU", scr("sc_xu", 1024), w_in[:, 1024:2048].rearrange("(k p) n -> p k n", p=128), 1024))
    pieces.append(("OUT", scr("sc_out", 1024), w_out.rearrange("(k p) n -> p k n", p=128), 1024))
    for p in range(4):
        pieces.append((f"UP{p}", scr(f"sc_up{p}", 1024),
                       w_up[:, 1024 * p:1024 * (p + 1)].rearrange("(k p) n -> p k n", p=128), 1024))
        pieces.append((f"DN{p}", scr(f"sc_dn{p}", 1024),
                       w_down[1024 * p:1024 * (p + 1), :].rearrange("(k p) n -> p k n", p=128), 1024))
    NPIECE = len(pieces)

    es = contextlib.ExitStack()
    with es:
        def sb(name, shape, dt):
            return es.enter_context(nc.sbuf_tensor(name, shape, dt))

        X = [sb(f"X{i}", [128, 8, MT], F32) for i in range(3)]
        HT = sb("HT", [128, 2, 8, MT], BF16)
        SQ = sb("SQ", [128, 8, MT], BF16)
        RS = [sb(f"RS{i}", [128, MT], F32) for i in range(3)]
        Z = sb("Z", [128, 4, MT + 4], F32)
        CT = [sb(f"CT{i}", [128, MT], F32) for i in range(2)]
        YA = sb("YA", [128, 4, MT], F32)
        T2 = [sb(f"T2{i}", [128, MT], F32) for i in range(2)]
        YB = sb("YB", [128, 4, MT], F32)
        VH = sb("VH", [128, 4, MT], BF16)
        ATY = sb("ATY", [128, 16 * MT], BF16)
        RL = [sb(f"RL{i}", [128, MT], F32) for i in range(3)]
        RING = [sb(f"RING{i}", [128, 8, 1024], BF16) for i in range(4)]
        PRM = sb("PRM", [128, NPRM], F32)
        BSF = sb("BSF", [128, 4, 128], F32)
        BIAS = sb("BIAS", [128, 4, 128], F32)
        WST = sb("WST", [128, 8, 128], BF16)
        ONESD = sb("ONESD", [128, 128], BF16)
        ONESE = sb("ONESE", [128, 128], BF16)
        ONE1 = sb("ONE1", [128, 64], BF16)
        EPST = sb("EPST", [128, 1], F32)
        LNS = sb("LNS", [128, 4, 6], F32)
        LNM = sb("LNM", [128, 4, 2], F32)
        LNR = sb("LNR", [128, 4], F32)
        LNB = sb("LNB", [128, 4], F32)
        PS = [es.enter_context(nc.psum_tensor(f"PS{i}", [128, MT], F32)) for i in range(8)]

        XH = YB[:, 0, 0:8 * HW].rearrange("p (k j) -> p k j", k=8)
        RSH = YB[:, 1, 0:HW]
        CH = YB[:, 2, 0:4 * HW].rearrange("p (c j) -> p c j", c=4)
        ZH = YB[:, 3, 0:4 * HW].rearrange("p (c j) -> p c j", c=4)
        SQH = VH[:, 0, 0:8 * HW].rearrange("p (k j) -> p k j", k=8)
        HH = VH[:, 1, 0:8 * HW].rearrange("p (k j) -> p k j", k=8)
        AT = ATY[:].rearrange("p (f t) -> p f t", f=8)
        YN = ATY[:].rearrange("p (m e t) -> p m e t", m=2, e=8)

        G1, G2, GF, GCONV, GGM, GLN, BLN, CW = 0, 8, 16, 24, 28, 32, 36, 40

        def pcol(base, k):
            return PRM[:, base + k:base + k + 1]

        state = {"bank": 0, "sbank": 0, "rs": 0, "ct": 0, "t2": 0, "rl": 0}

        def nxt(name, n):
            v = state[name]
            state[name] = (v + 1) % n
            return v

        NRING = 6

        def alloc_bank():
            return nxt("bank", NRING)

        def alloc_stat_bank():
            return NRING + nxt("sbank", 8 - NRING)

        def pe_group(bank, mms, reads, prio=1):
            def fn(e, mms=mms):
                ins = None
                for (o, l, r, st, sp) in mms:
                    ins = e.matmul(o, lhsT=l, rhs=r, start=st, stop=sp)
                return ins
            dur = 0.03 + sum(max(r.shape[-1], 64) for (_o, _l, r, _a, _b) in mms) / 2300.0
            S.add("pe", fn, reads=reads, writes=[("ps", bank)], dur=dur, prio=prio)

        def wk(slot, qs=(0, 1, 2, 3)):
            return [("W", slot)] + [("Wq", slot, q) for q in qs]

        def cast_piece(i):
            name, sc, src, n = pieces[i]
            S.add("pool", lambda e, sc=sc, src=src: e.dma_start(out=sc, in_=src),
                  writes=[("scr", i)], dma_key=("cast", i), nbytes=6 * 128 * 8 * n)

        def cast_quarter(i, q, dur=None, reads=(), tok=None):
            name, sc, src, n = pieces[i]
            slot = i % 4
            S.add("pool", lambda e: e.dma_start(out=RING[slot][:, :, 256 * q:256 * (q + 1)],
                                                in_=src[:, :, 256 * q:256 * (q + 1)]),
                  reads=list(reads), writes=[("Wq", slot, q)] + ([tok] if tok else []),
                  dma_key=("cast", i, q), nbytes=4 * 128 * 8 * 256, dur=dur)

        def save_piece(i):
            name, sc, src, n = pieces[i]
            slot = i % 4
            S.add("sp", lambda e: e.dma_start(out=sc, in_=RING[slot][:, :, 0:n]),
                  reads=wk(slot), writes=[("scr", i)], dma_key=("wsave", i), nbytes=2 * 128 * 8 * n)

        def cast_piece_to_ring(i, dur=None, reads=(), tok=None):
            name, sc, src, n = pieces[i]
            slot = i % 4
            S.add("pool", lambda e: e.dma_start(out=RING[slot][:, :, 0:n], in_=src),
                  reads=list(reads), writes=wk(slot) + ([tok] if tok else []),
                  dma_key=("cast", i), nbytes=4 * 128 * 8 * n, dur=dur)
            save_piece(i)

        def load_piece(i, g=1):
            if g == 0:
                return cast_piece_to_ring(i)
            name, sc, src, n = pieces[i]
            slot = i % 4
            S.add("sp", lambda e, sc=sc, slot=slot, n=n: e.dma_start(out=RING[slot][:, :, 0:n], in_=sc),
                  reads=[("scr", i)], writes=wk(slot), dma_key=("wl", slot), nbytes=2 * 128 * 8 * n)

        def load_x(m, reads=()):
            xb = m % 3
            for j in range(4):
                src = xT[m, :, 2 * MT * j:2 * MT * (j + 1)].rearrange("p (k t) -> p k t", k=2)
                S.add("sp", lambda e, xb=xb, src=src, j=j: e.dma_start(out=X[xb][:, 2 * j:2 * j + 2, :], in_=src),
                      reads=list(reads), writes=[("X", xb, 2 * j), ("X", xb, 2 * j + 1)], dma_key=("xl", xb, j),
                      nbytes=4 * 128 * 2 * MT)

        def store_x(m, j):
            xb = m % 3
            dst = oT[m, :, 2 * MT * j:2 * MT * (j + 1)].rearrange("p (k t) -> p k t", k=2)
            S.add("sp", lambda e, xb=xb, dst=dst, j=j: e.dma_start(out=dst, in_=X[xb][:, 2 * j:2 * j + 2, :]),
                  reads=[("X", xb, 2 * j), ("X", xb, 2 * j + 1)], writes=[("out", m, j)],
                  dma_key=("xs", xb, j), nbytes=4 * 128 * 2 * MT)

        def rstd_from_ms(out_ap, in_ap, reads, wkey, n):
            S.add("act", lambda e: e.activation(out=out_ap, in_=in_ap, func=AF.Ln, bias=EPST[:, :], scale=1.0),
                  reads=reads + [("const",)], writes=[wkey], n=n, prio=0)
            S.add("act", lambda e: e.activation(out=out_ap, in_=out_ap, func=AF.Exp, scale=-0.5),
                  reads=[wkey], writes=[wkey], n=n, prio=0)

        def rms_stats(src, src_keys, n, ones):
            bank = alloc_stat_bank()
            if n == 8:
                for j in range(4):
                    for c in (2 * j, 2 * j + 1):
                        S.add("act", lambda e, c=c: e.activation(out=SQ[:, c, :], in_=src[:, c, :], func=AF.Square),
                              reads=[src_keys[c]], writes=[("SQ", c)], prio=0)
                    S.add("dve", lambda e, j=j: e.tensor_tensor(out=SQ[:, 2 * j, :], in0=SQ[:, 2 * j, :],
                                                                in1=SQ[:, 2 * j + 1, :], op=ALU.add),
                          reads=[("SQ", 2 * j), ("SQ", 2 * j + 1)], writes=[("SQ", 2 * j)], prio=0, dur=0.42)
                    pe_group(bank, [(PS[bank][:, :], ones[:, :], SQ[:, 2 * j, :], j == 0, j == 3)],
                             reads=[("SQ", 2 * j), ("const",)], prio=0)
            else:
                for c in range(n):
                    S.add("act", lambda e, c=c: e.activation(out=SQ[:, c, :], in_=src[:, c, :], func=AF.Square),
                          reads=[src_keys[c]], writes=[("SQ", c)], prio=0)
                    pe_group(bank, [(PS[bank][:, :], ones[:, :], SQ[:, c, :], c == 0, c == n - 1)],
                             reads=[("SQ", c), ("const",)], prio=0)
            slot = nxt("rs", 3)
            rstd_from_ms(RS[slot][:, :], PS[bank][:, :], [("ps", bank)], ("RS", slot), MT)
            return slot

        def normalize(out_fn, src_fn, gbase, n, slot, in_keys, out_keys, pool_chunks):
            for k in range(n):
                eng = "pool" if k in pool_chunks else "dve"
                S.add(eng, lambda e, k=k: e.scalar_tensor_tensor(
                    out=out_fn(k), in0=src_fn(k), scalar=pcol(gbase, k), in1=RS[slot][:, :],
                    op0=ALU.mult, op1=ALU.mult),
                    reads=[in_keys[k], ("RS", slot), ("const",)], writes=[out_keys[k]], prio=0)

        def stage0(m):
            xb, hm = m % 3, m % 2
            xk = [("X", xb, k) for k in range(8)]
            slot = rms_stats(X[xb], xk, 8, ONESD)
            normalize(lambda k: HT[:, hm, k, :], lambda k: X[xb][:, k, :], G1, 8, slot, xk,
                      [("HT", hm, k) for k in range(8)], ())

        def halo_stage():
            S.add("act", lambda e: e.activation(out=SQH, in_=XH, func=AF.Square),
                  reads=[("YB", 0)], writes=[("VH", 0)])
            bank = alloc_bank()
            pe_group(bank, [(PS[bank][:, 0:HW], ONESD[:, :], SQH[:, k, :], k == 0, k == 7) for k in range(8)],
                     reads=[("VH", 0), ("const",)])
            rstd_from_ms(RSH, PS[bank][:, 0:HW], [("ps", bank)], ("YB", 1), HW)
            for k in range(8):
                S.add("dve", lambda e, k=k: e.scalar_tensor_tensor(
                    out=HH[:, k, :], in0=XH[:, k, :], scalar=pcol(G1, k), in1=RSH,
                    op0=ALU.mult, op1=ALU.mult),
                    reads=[("YB", 0), ("YB", 1), ("const",)], writes=[("VH", 1)])
            bank2 = alloc_bank()
            WBC, WXU = RING[1], RING[2]
            mms = []
            for c in range(4):
                for k in range(8):
                    mms.append((PS[bank2][:, 2 * HW * c:2 * HW * c + HW],
                                WBC[:, k, 512 + c * 128:512 + (c + 1) * 128], HH[:, k, :], k == 0, k == 7))
                for k in range(8):
                    mms.append((PS[bank2][:, 2 * HW * c + HW:2 * HW * (c + 1)],
                                WXU[:, k, c * 128:(c + 1) * 128], HH[:, k, :], k == 0, k == 7))
            pe_group(bank2, mms, reads=[("VH", 1)] + wk(1, (2, 3)) + wk(2, (0, 1)))
            psv = PS[bank2][:, 0:8 * HW].rearrange("p (c j) -> p c j", c=4)
            S.add("act", lambda e: e.activation(out=CH, in_=psv[:, :, 0:HW], func=AF.Identity),
                  reads=[("ps", bank2)], writes=[("YB", 2)])
            S.add("dve", lambda e: e.tensor_tensor(out=ZH, in0=psv[:, :, HW:2 * HW], in1=CH,
                                                   op=ALU.mult),
                  reads=[("ps", bank2), ("YB", 2)], writes=[("YB", 3)])
            S.add("dve", lambda e: e.tensor_copy(out=Z[:, :, 0:2], in_=ZH[:, :, HW - 2:HW]),
                  reads=[("YB", 3)], writes=[("Z", c) for c in range(4)])

        def stage1_v(m):
            hm = m % 2
            WV = RING[0]
            banks = []
            for t in range(4):
                bank = alloc_bank()
                banks.append(bank)
                pe_group(bank, [(PS[bank][:, :], HT[:, hm, k, t * 128:(t + 1) * 128], WV[:, k, 0:512],
                                 k == 0, k == 7) for k in range(8)],
                         reads=[("HT", hm, k) for k in range(8)] + wk(0))
                r1, r2 = nxt("rl", 3), nxt("rl", 3)
                S.add("act", lambda e, t=t, bank=bank, r1=r1: e.activation(
                    out=RL[r1][:, :], in_=PS[bank][:, :], func=AF.Identity, accum_out=LNS[:, 0, t:t + 1]),
                    reads=[("ps", bank)], writes=[("RL", r1), ("LNS", 0, t)], prio=0)
                S.add("act", lambda e, t=t, bank=bank, r2=r2: e.activation(
                    out=RL[r2][:, :], in_=PS[bank][:, :], func=AF.Square, accum_out=LNS[:, 1, t:t + 1]),
                    reads=[("ps", bank)], writes=[("RL", r2), ("LNS", 1, t)], prio=0)
            skeys = [("LNS", a, t) for a in range(2) for t in range(4)]
            mkeys = [("LNM", t) for t in range(4)]
            S.add("dve", lambda e: e.tensor_scalar(out=LNM[:, :, 0], in0=LNS[:, 0, 0:4], scalar1=1.0 / 512.0,
                                                   scalar2=None, op0=ALU.mult),
                  reads=skeys, writes=mkeys, n=4, prio=0)
            S.add("dve", lambda e: e.tensor_tensor(out=LNS[:, 2, 0:4], in0=LNM[:, :, 0], in1=LNM[:, :, 0],
                                                   op=ALU.mult),
                  reads=mkeys, writes=[("LNS", 2)], n=4, prio=0)
            S.add("dve", lambda e: e.scalar_tensor_tensor(out=LNM[:, :, 1], in0=LNS[:, 1, 0:4], scalar=1.0 / 512.0,
                                                          in1=LNS[:, 2, 0:4], op0=ALU.mult, op1=ALU.subtract),
                  reads=skeys + [("LNS", 2)], writes=mkeys, n=4, prio=0)
            rstd_from_ms(LNR[:, :], LNM[:, :, 1], mkeys, ("LNR",), 4)
            S.add("dve", lambda e: e.scalar_tensor_tensor(out=LNB[:, :], in0=LNM[:, :, 0], scalar=-1.0,
                                                          in1=LNR[:, :], op0=ALU.mult, op1=ALU.mult),
                  reads=[("LNR",)] + [("LNM", t) for t in range(4)], writes=[("LNB",)])
            for t in range(4):
                bank = banks[t]
                S.add("act", lambda e, t=t, bank=bank: e.activation(
                    out=VH[:, t, :], in_=PS[bank][:, :], func=AF.Identity,
                    bias=LNB[:, t:t + 1], scale=LNR[:, t:t + 1]),
                    reads=[("ps", bank), ("LNR",), ("LNB",)], writes=[("VH", t)])

        def stage1_conv(m):
            hm = m % 2
            WBC, WXU = RING[1], RING[2]
            hreads = [("HT", hm, k) for k in range(8)]
            for c in range(4):
                bC, bX, bB = alloc_bank(), alloc_bank(), alloc_bank()
                pe_group(bC, [(PS[bC][:, :], WBC[:, k, 512 + c * 128:512 + (c + 1) * 128], HT[:, hm, k, :],
                               k == 0, k == 7) for k in range(8)], reads=hreads + wk(1, (2 + c // 2,)))
                pe_group(bX, [(PS[bX][:, :], WXU[:, k, c * 128:(c + 1) * 128], HT[:, hm, k, :],
                               k == 0, k == 7) for k in range(8)], reads=hreads + wk(2, (c // 2,)))
                pe_group(bB, [(PS[bB][:, :], WBC[:, k, c * 128:(c + 1) * 128], HT[:, hm, k, :],
                               k == 0, k == 7) for k in range(8)], reads=hreads + wk(1, (c // 2,)))
                ci = nxt("ct", 2)
                S.add("act", lambda e, ci=ci, bC=bC: e.activation(out=CT[ci][:, :], in_=PS[bC][:, :], func=AF.Identity),
                      reads=[("ps", bC)], writes=[("CT", ci)])
                S.add("dve", lambda e, c=c, ci=ci, bX=bX: e.tensor_tensor(
                    out=Z[:, c, 2:2 + MT], in0=PS[bX][:, :], in1=CT[ci][:, :], op=ALU.mult),
                    reads=[("ps", bX), ("CT", ci)], writes=[("Z", c)])
                S.add("act", lambda e, c=c: e.activation(out=YA[:, c, :], in_=Z[:, c, 2:2 + MT], func=AF.Identity,
                                                         scale=pcol(CW, 3 * c + 2)),
                      reads=[("Z", c), ("const",)], writes=[("YA", c)])
                S.add("dve", lambda e, c=c: e.scalar_tensor_tensor(
                    out=YA[:, c, :], in0=Z[:, c, 1:1 + MT], scalar=pcol(CW, 3 * c + 1), in1=YA[:, c, :],
                    op0=ALU.mult, op1=ALU.add),
                    reads=[("Z", c), ("YA", c), ("const",)], writes=[("YA", c)])
                S.add("dve", lambda e, c=c: e.scalar_tensor_tensor(
                    out=YA[:, c, :], in0=Z[:, c, 0:MT], scalar=pcol(CW, 3 * c + 0), in1=YA[:, c, :],
                    op0=ALU.mult, op1=ALU.add),
                    reads=[("Z", c), ("YA", c), ("const",)], writes=[("YA", c)])
                S.add("dve", lambda e, c=c, bB=bB: e.tensor_tensor(
                    out=YA[:, c, :], in0=PS[bB][:, :], in1=YA[:, c, :], op=ALU.mult),
                    reads=[("ps", bB), ("YA", c)], writes=[("YA", c)])
                S.add("pool", lambda e, c=c: e.tensor_copy(out=Z[:, c, 0:2], in_=Z[:, c, MT:MT + 2]),
                      reads=[("Z", c)], writes=[("Z", c)], dur=0.3)

        def stage1_su(m):
            hm = m % 2
            WXU = RING[2]
            hreads = [("HT", hm, k) for k in range(8)]
            for c in range(4):
                bS, bU = alloc_bank(), alloc_bank()
                mms = []
                for t in range(4):
                    for hh in range(2):
                        h = 2 * c + hh
                        mms.append((PS[bS][64 * hh:64 * hh + 64, t * 128:(t + 1) * 128],
                                    VH[:, t, h * 64:(h + 1) * 64], WST[:, h, :], True, True))
                pe_group(bS, mms, reads=[("VH", t) for t in range(4)] + [("WST",)])
                pe_group(bU, [(PS[bU][:, :], WXU[:, k, 512 + c * 128:512 + (c + 1) * 128], HT[:, hm, k, :],
                               k == 0, k == 7) for k in range(8)], reads=hreads + wk(2, (2 + c // 2,)))
                ti = nxt("t2", 2)
                bias_b = BIAS[:, c, :].unsqueeze(1).broadcast_to([128, 4, 128])
                S.add("dve", lambda e, c=c, ti=ti, bS=bS, bias_b=bias_b: e.scalar_tensor_tensor(
                    out=T2[ti][:, :].rearrange("p (t i) -> p t i", t=4),
                    in0=PS[bS][:, :].rearrange("p (t i) -> p t i", t=4),
                    scalar=pcol(GLN, c), in1=bias_b, op0=ALU.mult, op1=ALU.add),
                    reads=[("ps", bS), ("BIAS",), ("const",)], writes=[("T2", ti)])
                S.add("dve", lambda e, c=c, ti=ti, bU=bU: e.tensor_tensor(
                    out=YB[:, c, :], in0=PS[bU][:, :], in1=T2[ti][:, :], op=ALU.mult),
                    reads=[("ps", bU), ("T2", ti)], writes=[("YB", c)])

        def y_norm(m, which):
            mm_ = m % 2
            if which == "a":
                src, key, gbase, ebase = YA, "YA", GCONV, 0
            else:
                src, key, gbase, ebase = YB, "YB", GGM, 4
            yk = [(key, c) for c in range(4)]
            slot = rms_stats(src, yk, 4, ONESE)
            normalize(lambda c: YN[:, mm_, ebase + c, :], lambda c: src[:, c, :], gbase, 4, slot, yk,
                      [("ATY", mm_ * 8 + ebase + c) for c in range(4)], ())

        def out_proj(m):
            xb, mm_ = m % 3, m % 2
            WO = RING[3]
            for dc in range(8):
                bank = alloc_bank()
                pe_group(bank, [(PS[bank][:, :], WO[:, e_, dc * 128:(dc + 1) * 128], YN[:, mm_, e_, :],
                                 e_ == 0, e_ == 7) for e_ in range(8)],
                         reads=[("ATY", mm_ * 8 + e_) for e_ in range(8)] + wk(3))
                S.add("dve", lambda e, dc=dc, bank=bank: e.tensor_tensor(
                    out=X[xb][:, dc, :], in0=PS[bank][:, :], in1=X[xb][:, dc, :], op=ALU.add),
                    reads=[("ps", bank), ("X", xb, dc)], writes=[("X", xb, dc)])

        def norm2(m):
            xb, hm = m % 3, m % 2
            xk = [("X", xb, k) for k in range(8)]
            slot = rms_stats(X[xb], xk, 8, ONESD)
            normalize(lambda k: HT[:, hm, k, :], lambda k: X[xb][:, k, :], G2, 8, slot, xk,
                      [("HT", hm, k) for k in range(8)], ())

        def ffn_up(m, p):
            hm = m % 2
            slot = (4 + 2 * p) % 4
            WU = RING[slot]
            hreads = [("HT", hm, k) for k in range(8)]
            for fc in range(8):
                bank = alloc_bank()
                pe_group(bank, [(PS[bank][:, :], WU[:, k, fc * 128:(fc + 1) * 128], HT[:, hm, k, :],
                                 k == 0, k == 7) for k in range(8)], reads=hreads + wk(slot))
                ri = nxt("rl", 3)
                S.add("act", lambda e, ri=ri, bank=bank: e.activation(out=RL[ri][:, :], in_=PS[bank][:, :],
                                                                      func=AF.Relu),
                      reads=[("ps", bank)], writes=[("RL", ri)])
                S.add("act", lambda e, ri=ri, fc=fc: e.activation(
                    out=AT[:, fc, hm * MT:(hm + 1) * MT], in_=RL[ri][:, :], func=AF.Square),
                    reads=[("RL", ri)], writes=[("ATY", fc * 2 + hm)])

        def ffn_down(m, p):
            xb, hm = m % 3, m % 2
            slot = (5 + 2 * p) % 4
            WD = RING[slot]
            for dc in range(8):
                bank = alloc_bank()
                pe_group(bank, [(PS[bank][:, :], WD[:, fc, dc * 128:(dc + 1) * 128],
                                 AT[:, fc, hm * MT:(hm + 1) * MT], fc == 0, fc == 7) for fc in range(8)],
                         reads=[("ATY", fc * 2 + hm) for fc in range(8)] + wk(slot))
                S.add("dve", lambda e, dc=dc, bank=bank: e.tensor_tensor(
                    out=X[xb][:, dc, :], in0=PS[bank][:, :], in1=X[xb][:, dc, :], op=ALU.add),
                    reads=[("ps", bank), ("X", xb, dc)], writes=[("X", xb, dc)])

        def final(m):
            xb = m % 3
            xk = [("X", xb, k) for k in range(8)]
            slot = rms_stats(X[xb], xk, 8, ONESD)
            normalize(lambda k: X[xb][:, k, :], lambda k: X[xb][:, k, :], GF, 8, slot, xk, xk, ())
            for j in range(4):
                store_x(m, j)

        wv = 3.0 + 2.1e6 / 160e3
        tokv = [("tok", 9)]
        cast_piece_to_ring(0, dur=wv, tok=tokv[0])
        w0 = 3.0 + 3.2e6 / 200e3
        toks = [("tok", j) for j in range(3)]
        for j, (i, q) in enumerate(((2, 0), (1, 2), (1, 0))):
            cast_quarter(i, q, dur=w0, reads=tokv, tok=toks[j])
        w1 = 3.0 + 3.2e6 / 200e3
        toks1 = [("tok", 4 + j) for j in range(3)]
        for j, (i, q) in enumerate(((2, 1), (1, 3), (1, 1))):
            cast_quarter(i, q, dur=w1, reads=toks, tok=toks1[j])
        w2 = 3.0 + 6.3e6 / 220e3
        for (i, q) in ((2, 2), (2, 3)):
            cast_quarter(i, q, dur=w2, reads=toks1)
        save_piece(1)
        save_piece(2)
        cast_piece_to_ring(3, dur=w2, reads=toks1)
        S.add("sp", lambda e: e.dma_start(out=PRM[:, :], in_=prm_d), writes=[("prm",)], dma_key=("misc", 0), prio=0)
        S.add("sp", lambda e: e.dma_start(out=BSF[:, :, :], in_=bsf_d), writes=[("bsf",)], dma_key=("misc", 1), prio=0)
        WSF = [RL[0][:, :].rearrange("p (h i) -> p h i", h=4), RL[1][:, :].rearrange("p (h i) -> p h i", h=4)]
        S.add("sp", lambda e: e.dma_start(out=WSF[0], in_=wst_d[:, 0:4, :]), writes=[("RL", 0)], dma_key=("misc", 2), prio=0)
        S.add("sp", lambda e: e.dma_start(out=WSF[1], in_=wst_d[:, 4:8, :]), writes=[("RL", 1)], dma_key=("misc", 3), prio=0)
        S.add("sp", lambda e: e.dma_start(out=XH, in_=xh), writes=[("YB", 0)], dma_key=("misc", 4), prio=0)
        load_x(0)
        load_x(1, reads=toks1)

        def consts(e):
            e.memset(ONESD[:, :], 1.0 / D)
            e.memset(ONESE[:, :], 1.0 / 512.0)
            e.memset(ONE1[:, :], 1.0)
            return e.memset(EPST[:, :], EPS)
        S.add("dve", consts, writes=[("const0",)])
        S.add("dve", lambda e: e.tensor_copy(out=WST[:, 0:4, :], in_=WSF[0]),
              reads=[("RL", 0), ("const0",), ("prm",)], writes=[("WST",), ("const",)])
        S.add("dve", lambda e: e.tensor_copy(out=WST[:, 4:8, :], in_=WSF[1]),
              reads=[("RL", 1)], writes=[("WST",)])
        S.add("dve", lambda e: e.memset(WST[64:128, :, 0:64], 0.0), reads=[("WST",)], writes=[("WST",)])
        bank = alloc_bank()
        mms = []
        for c in range(4):
            for hh in range(2):
                mms.append((PS[bank][64 * hh:64 * hh + 64, c * 128:(c + 1) * 128], ONE1[:, :],
                            WST[:, 2 * c + hh, :], True, True))
        pe_group(bank, mms, reads=[("WST",), ("const",)])
        S.add("dve", lambda e: e.tensor_tensor(out=BIAS[:, :, :],
                                               in0=PS[bank][:, :].rearrange("p (c i) -> p c i", c=4),
                                               in1=PRM[:, BLN:BLN + 4].unsqueeze(2).broadcast_to([128, 4, 128]),
                                               op=ALU.mult),
              reads=[("ps", bank), ("const",)], writes=[("BIAS",)])
        S.add("dve", lambda e: e.tensor_tensor(out=BIAS[:, :, :], in0=BIAS[:, :, :], in1=BSF[:, :, :], op=ALU.add),
              reads=[("BIAS",), ("bsf",)], writes=[("BIAS",)])

        stage0(0)
        halo_stage()
        if debug:
            S.add("sp", lambda e: e.dma_start(out=dbg_z, in_=Z[:, :, 0:2]), reads=[("Z", c) for c in range(4)],
                  writes=[("dbg", 0)], dma_key=("dbg", 0))
            S.add("sp", lambda e: e.dma_start(out=dbg_ch, in_=CH), reads=[("YB", 2)],
                  writes=[("dbg", 1)], dma_key=("dbg", 1))
            S.add("sp", lambda e: e.dma_start(out=dbg_rsh, in_=RSH), reads=[("YB", 1)],
                  writes=[("dbg", 2)], dma_key=("dbg", 2))
            S.add("sp", lambda e: e.dma_start(out=dbg_hh, in_=HH), reads=[("VH", 1)],
                  writes=[("dbg", 3)], dma_key=("dbg", 3))

        for g in range(NGRP):
            mA, mB = 2 * g, 2 * g + 1
            last = g == NGRP - 1
            stage1_v(mA)
            stage1_conv(mA)
            stage1_su(mA)
            stage0(mB)
            stage1_v(mB)
            load_piece(4, g)
            y_norm(mA, "a")
            stage1_conv(mB)
            load_piece(5, g)
            y_norm(mA, "b")
            stage1_su(mB)
            load_piece(6, g)
            out_proj(mA)
            y_norm(mB, "a")
            y_norm(mB, "b")
            norm2(mA)
            out_proj(mB)
            load_piece(7, g)
            if not last:
                load_x(mA + 2)
            norm2(mB)
            for p in range(4):
                ffn_up(mA, p)
                ffn_up(mB, p)
                idx = 8 + 2 * p
                if idx < NPIECE:
                    load_piece(idx, g)
                elif not last:
                    load_piece(idx - NPIECE)
                if p == 3 and not last:
                    stage0(mA + 2)
                ffn_down(mA, p)
                ffn_down(mB, p)
                idx = 9 + 2 * p
                if idx < NPIECE:
                    load_piece(idx, g)
                elif not last:
                    load_piece(idx - NPIECE)
            final(mA)
            final(mB)
            if not last:
                load_x(mB + 2)

        S.schedule()
        dma_keys = S.resolve()
        eng_sems = {e_: es.enter_context(nc.semaphore(f"sem_{e_}")) for e_ in ENGS}
        dma_sems = {k: es.enter_context(nc.semaphore("dma_" + "_".join(str(x) for x in k))) for k in dma_keys}
        S.assign(eng_sems, dma_sems)

        block = es.enter_context(nc.Block())

        @block.tensor
        def _(e):
            S.emit("pe", e)

        @block.scalar
        def _(e):
            S.emit("act", e)

        @block.vector
        def _(e):
            S.emit("dve", e)

        @block.gpsimd
        def _(e):
            S.emit("pool", e)

        @block.sync
        def _(e):
            S.emit("sp", e)
            for k in dma_sems:
                if k[0] == "xs":
                    e.wait_ge(dma_sems[k], S.final_dma[k])
    return nc


_CACHE = {}


def _host_inputs(inp):
    f = lambda a: np.ascontiguousarray(np.asarray(a, dtype=np.float32))
    x = f(inp["x"])
    prm = np.zeros((128, NPRM), np.float32)

    def cols(v, n):
        return f(v).reshape(n, 128).T

    prm[:, 0:8] = cols(inp["norm1_g"][0], 8)
    prm[:, 8:16] = cols(inp["norm2_g"][0], 8)
    prm[:, 16:24] = cols(inp["final_g"], 8)
    prm[:, 24:28] = cols(inp["out_norm_conv_g"][0], 4)
    prm[:, 28:32] = cols(inp["out_norm_gmlp_g"][0], 4)
    prm[:, 32:36] = cols(inp["gmlp_ln_g"][0], 4)
    prm[:, 36:40] = cols(inp["gmlp_ln_b"][0], 4)
    cw = f(inp["conv_w"][0]).reshape(4, 128, 3).transpose(1, 0, 2).reshape(128, 12)
    prm[:, 40:52] = cw
    bs = f(inp["gmlp_bs"][0])
    bsf = np.repeat(bs.reshape(4, 2, 1, 128), 64, axis=2).reshape(4, 128, 128).transpose(1, 0, 2)
    wst = f(inp["gmlp_ws"][0]).transpose(2, 0, 1)
    shared = {
        "prm": np.ascontiguousarray(prm),
        "bsf": np.ascontiguousarray(bsf),
        "wst": np.ascontiguousarray(wst),
        "w_in": f(inp["w_in"][0]),
        "w_out": f(inp["w_out"][0]),
        "w_up": f(inp["w_up"][0]),
        "w_down": f(inp["w_down"][0]),
    }
    maps = []
    for c in range(NCORES):
        b, half = c // 2, c % 2
        t0 = half * TOK
        xc = x[b, t0:t0 + TOK, :]
        xh = np.zeros((HW, D), np.float32)
        if half == 1:
            xh[HW - 2:HW] = x[b, t0 - 2:t0, :]
        m = dict(shared)
        m["xT"] = np.ascontiguousarray(xc.reshape(NMT, MT, 8, 128).transpose(0, 3, 2, 1)).reshape(NMT, 128, 8 * MT)
        m["xh"] = np.ascontiguousarray(xh.T.reshape(8, 128, HW).transpose(1, 0, 2))
        maps.append(m)
    return maps


def kernel(**inputs):
    if "nc" not in _CACHE:
        _CACHE["nc"] = build_program()
    nc = _CACHE["nc"]
    maps = _host_inputs(inputs)
    res = run_bass_kernel_spmd(nc, maps, core_ids=list(range(NCORES)))
    out = np.empty((BATCH, SEQ, D), np.float32)
    for c in range(NCORES):
        b, half = c // 2, c % 2
        o = np.asarray(res.results[c]["oT"]).reshape(NMT, 128, 8, MT)
        out[b, half * TOK:(half + 1) * TOK, :] = o.transpose(0, 3, 2, 1).reshape(TOK, D)
    return out
```

```python
import contextlib
import numpy as np
import concourse.bass as bass
import concourse.mybir as mybir
from concourse.bass_utils import run_bass_kernel_spmd

F32 = mybir.dt.float32
BF16 = mybir.dt.bfloat16
ALU = mybir.AluOpType
AF = mybir.ActivationFunctionType

NCORES = 8
D = 1024
SEQ = 8192
BATCH = 4
TOK = SEQ * BATCH // NCORES
MT = 512
NMT = TOK // MT
NGRP = NMT // 2
EPS = 1e-6
NPRM = 52
HW = 32

ENGS = ("pe", "act", "dve", "pool", "sp")


class Op:
    __slots__ = ("eng", "fn", "dma_key", "deps", "needs_inc", "sem", "semval", "waits",
                 "idx", "dur", "preds", "succs", "npred", "ready_t", "start", "finish", "bl", "prio")

    def __init__(self, eng, fn, dma_key, idx, dur):
        self.eng = eng
        self.fn = fn
        self.dma_key = dma_key
        self.idx = idx
        self.dur = dur
        self.deps = []
        self.needs_inc = False
        self.sem = None
        self.semval = None
        self.waits = []


class Sched:
    def __init__(self):
        self.ops = []
        self.by_eng = {e: [] for e in ENGS}
        self.last_w = {}
        self.readers = {}
        self.two_pass = False
        self.event_order = False

    def add(self, eng, fn, reads=(), writes=(), dma_key=None, dur=None, n=512, nbytes=0, prio=1):
        if dur is None:
            if dma_key is not None:
                dur = 4.0 + nbytes / 80e3
            elif eng == "act":
                dur = (n + 335) / 1200.0
            elif eng == "dve":
                dur = (n + 151) / 960.0
            elif eng == "pool":
                dur = (2.2 * n + 160) / 1000.0
            else:
                dur = 0.25
        op = Op(eng, fn, dma_key, len(self.ops), dur)
        op.prio = prio
        for k in reads:
            w = self.last_w.get(k)
            if w is not None:
                op.deps.append((w, "RAW"))
        for k in writes:
            w = self.last_w.get(k)
            if w is not None:
                op.deps.append((w, "WAW"))
            for r in self.readers.get(k, ()):
                op.deps.append((r, "WAR"))
        for k in reads:
            self.readers.setdefault(k, []).append(op)
        for k in writes:
            self.last_w[k] = op
            self.readers[k] = []
        self.ops.append(op)
        self.by_eng[eng].append(op)
        return op

    def schedule(self, lat=0.2):
        ops = self.ops
        for op in ops:
            preds = {}
            for d, _kind in op.deps:
                if d is not op:
                    preds[id(d)] = d
            op.preds = list(preds.values())
            op.succs = []
        for op in ops:
            for d in op.preds:
                d.succs.append(op)
        for op in reversed(ops):
            b = 0.0
            for sc in op.succs:
                if sc.bl + lat > b:
                    b = sc.bl + lat
            op.bl = b + op.dur

        def run(keyfn, window):
            for op in ops:
                op.npred = len(op.preds)
                op.ready_t = 0.0
            ready = {e: [] for e in ENGS}
            for op in ops:
                if op.npred == 0:
                    ready[op.eng].append(op)
            t_free = {e: 0.0 for e in ENGS}
            order = []
            while True:
                best = None
                for e in ENGS:
                    lst = ready[e]
                    if not lst:
                        continue
                    tf = t_free[e]
                    horizon = max(tf, min(o.ready_t for o in lst)) + window
                    cand = None
                    for op in lst:
                        if op.ready_t <= horizon:
                            k = keyfn(op)
                            if cand is None or k < cand[0]:
                                cand = (k, op)
                    op = cand[1]
                    st = op.ready_t if op.ready_t > tf else tf
                    if best is None or (st, op.idx) < best[0]:
                        best = ((st, op.idx), op)
                if best is None:
                    break
                (st, _), op = best
                ready[op.eng].remove(op)
                op.start = st
                op.finish = st + op.dur
                t_free[op.eng] = st + (0.1 if op.dma_key is not None else op.dur)
                order.append(op)
                for sc in op.succs:
                    sc.npred -= 1
                    rt = op.finish + lat
                    if rt > sc.ready_t:
                        sc.ready_t = rt
                    if sc.npred == 0:
                        ready[sc.eng].append(sc)
            assert len(order) == len(ops)
            return order

        order = run(lambda op: (op.prio, -op.bl, op.idx), 0.0)
        if self.two_pass:
            total = max(o.finish for o in order)
            ls = {}
            for op in reversed(order):
                v = total
                for sc in op.succs:
                    c = ls[id(sc)] - lat
                    if c < v:
                        v = c
                v -= op.dur
                if op.eng == "pe" and op.start < v:
                    v = op.start
                ls[id(op)] = v
            order = run(lambda op: (ls[id(op)], op.idx), 0.4)
        if self.event_order:
            ev = {}
            for op in order:
                v = 0.0
                for p in op.preds:
                    c = p.finish if (p.eng == "pe" or p.dma_key is not None) else ev[id(p)]
                    if c > v:
                        v = c
                ev[id(op)] = v
            key = {}
            for op in order:
                if op.eng == "pe":
                    key[id(op)] = (op.start, op.start, op.idx)
                else:
                    key[id(op)] = (ev[id(op)], op.start, op.idx)
            order = sorted(order, key=lambda o: key[id(o)])
            by_eng = {e: [o for o in order if o.eng == e] for e in ENGS}
            pos = {e: 0 for e in ENGS}
            t_free = {e: 0.0 for e in ENGS}
            done = set()
            n_done = 0
            while n_done < len(order):
                progressed = False
                for e in ENGS:
                    while pos[e] < len(by_eng[e]):
                        op = by_eng[e][pos[e]]
                        if any(id(p) not in done for p in op.preds):
                            break
                        st = t_free[e]
                        for p in op.preds:
                            if p.finish + lat > st:
                                st = p.finish + lat
                        op.start = st
                        op.finish = st + op.dur
                        t_free[e] = st + (0.1 if op.dma_key is not None else op.dur)
                        done.add(id(op))
                        pos[e] += 1
                        n_done += 1
                        progressed = True
                assert progressed, "engine queues deadlock"
        self.ops = order
        self.by_eng = {e: [o for o in order if o.eng == e] for e in ENGS}
        self.est_total = max(o.finish for o in order)

    def resolve(self):
        for op in self.ops:
            need = {}
            for d, kind in op.deps:
                if d is op:
                    continue
                if d.dma_key is None and d.eng == op.eng and op.eng == "pe":
                    continue
                need[id(d)] = d
            op.waits = list(need.values())
            for d in op.waits:
                d.needs_inc = True
        dma_keys = []
        for op in self.ops:
            if op.dma_key is not None:
                op.needs_inc = True
                if op.dma_key not in dma_keys:
                    dma_keys.append(op.dma_key)
        return dma_keys

    def assign(self, eng_sems, dma_sems):
        cnt = {e: 0 for e in ENGS}
        dcnt = {}
        for op in self.ops:
            if op.dma_key is not None:
                dcnt[op.dma_key] = dcnt.get(op.dma_key, 0) + 16
                op.sem = dma_sems[op.dma_key]
                op.semval = dcnt[op.dma_key]
            elif op.needs_inc:
                cnt[op.eng] += 1
                op.sem = eng_sems[op.eng]
                op.semval = cnt[op.eng]
        self.final_dma = dict(dcnt)

    def emit(self, eng, e):
        waited = {}
        for op in self.by_eng[eng]:
            best = {}
            for d in op.waits:
                key = id(d.sem)
                if d.semval > best.get(key, (None, 0))[1]:
                    best[key] = (d.sem, d.semval)
            for key, (sem, val) in best.items():
                if waited.get(key, 0) >= val:
                    continue
                e.wait_ge(sem, val)
                waited[key] = val
            ins = op.fn(e)
            if op.dma_key is not None:
                ins.then_inc(op.sem, 16)
            elif op.needs_inc:
                ins.then_inc(op.sem, 1)


def build_program(debug=False):
    nc = bass.Bass("TRN2", target_bir_lowering=False)
    S = Sched()
    if debug:
        dbg_z = nc.dram_tensor("dbg_z", [128, 4, 2], F32, kind="ExternalOutput").ap()
        dbg_ch = nc.dram_tensor("dbg_ch", [128, 4, HW], F32, kind="ExternalOutput").ap()
        dbg_rsh = nc.dram_tensor("dbg_rsh", [128, HW], F32, kind="ExternalOutput").ap()
        dbg_hh = nc.dram_tensor("dbg_hh", [128, 8, HW], BF16, kind="ExternalOutput").ap()

    xT = nc.dram_tensor("xT", [NMT, 128, 8 * MT], F32, kind="ExternalInput").ap()
    xh = nc.dram_tensor("xh", [128, 8, HW], F32, kind="ExternalInput").ap()
    prm_d = nc.dram_tensor("prm", [128, NPRM], F32, kind="ExternalInput").ap()
    bsf_d = nc.dram_tensor("bsf", [128, 4, 128], F32, kind="ExternalInput").ap()
    wst_d = nc.dram_tensor("wst", [128, 8, 128], F32, kind="ExternalInput").ap()
    w_in = nc.dram_tensor("w_in", [D, 2560], F32, kind="ExternalInput").ap()
    w_out = nc.dram_tensor("w_out", [D, D], F32, kind="ExternalInput").ap()
    w_up = nc.dram_tensor("w_up", [D, 4096], F32, kind="ExternalInput").ap()
    w_down = nc.dram_tensor("w_down", [4096, D], F32, kind="ExternalInput").ap()
    oT = nc.dram_tensor("oT", [NMT, 128, 8 * MT], F32, kind="ExternalOutput").ap()

    def scr(name, n):
        return nc.dram_tensor(name, [128, 8, n], BF16, kind="Internal").ap()

    pieces = []
    pieces.append(("V", scr("sc_v", 512), w_in[:, 2048:2560].rearrange("(k p) n -> p k n", p=128), 512))
    pieces.append(("BC", scr("sc_bc", 1024), w_in[:, 0:1024].rearrange("(k p) n -> p k n", p=128), 1024))
    pieces.append(("XU", scr("sc_xu", 1024), w_in[:, 1024:2048].rearrange("(k p) n -> p k n", p=128), 1024))
    pieces.append(("OUT", scr("sc_out", 1024), w_out.rearrange("(k p) n -> p k n", p=128), 1024))
    for p in range(4):
        pieces.append((f"UP{p}", scr(f"sc_up{p}", 1024),
                       w_up[:, 1024 * p:1024 * (p + 1)].rearrange("(k p) n -> p k n", p=128), 1024))
        pieces.append((f"DN{p}", scr(f"sc_dn{p}", 1024),
                       w_down[1024 * p:1024 * (p + 1), :].rearrange("(k p) n -> p k n", p=128), 1024))
    NPIECE = len(pieces)

    es = contextlib.ExitStack()
    with es:
        def sb(name, shape, dt):
            return es.enter_context(nc.sbuf_tensor(name, shape, dt))

        X = [sb(f"X{i}", [128, 8, MT], F32) for i in range(3)]
        HT = sb("HT", [128, 2, 8, MT], BF16)
        SQ = sb("SQ", [128, 8, MT], BF16)
        RS = [sb(f"RS{i}", [128, MT], F32) for i in range(3)]
        Z = sb("Z", [128, 4, MT + 4], F32)
        CT = [sb(f"CT{i}", [128, MT], F32) for i in range(2)]
        YA = sb("YA", [128, 4, MT], F32)
        T2 = [sb(f"T2{i}", [128, MT], F32) for i in range(2)]
        YB = sb("YB", [128, 4, MT], F32)
        VH = sb("VH", [128, 4, MT], BF16)
        ATY = sb("ATY", [128, 16 * MT], BF16)
        RL = [sb(f"RL{i}", [128, MT], F32) for i in range(3)]
        RING = [sb(f"RING{i}", [128, 8, 1024], BF16) for i in range(4)]
        PRM = sb("PRM", [128, NPRM], F32)
        BSF = sb("BSF", [128, 4, 128], F32)
        BIAS = sb("BIAS", [128, 4, 128], F32)
        WST = sb("WST", [128, 8, 128], BF16)
        ONESD = sb("ONESD", [128, 128], BF16)
        ONESE = sb("ONESE", [128, 128], BF16)
        ONE1 = sb("ONE1", [128, 64], BF16)
        EPST = sb("EPST", [128, 1], F32)
        LNS = sb("LNS", [128, 4, 6], F32)
        LNM = sb("LNM", [128, 4, 2], F32)
        LNR = sb("LNR", [128, 4], F32)
        LNB = sb("LNB", [128, 4], F32)
        PS = [es.enter_context(nc.psum_tensor(f"PS{i}", [128, MT], F32)) for i in range(8)]

        XH = YB[:, 0, 0:8 * HW].rearrange("p (k j) -> p k j", k=8)
        RSH = YB[:, 1, 0:HW]
        CH = YB[:, 2, 0:4 * HW].rearrange("p (c j) -> p c j", c=4)
        ZH = YB[:, 3, 0:4 * HW].rearrange("p (c j) -> p c j", c=4)
        SQH = VH[:, 0, 0:8 * HW].rearrange("p (k j) -> p k j", k=8)
        HH = VH[:, 1, 0:8 * HW].rearrange("p (k j) -> p k j", k=8)
        AT = ATY[:].rearrange("p (f t) -> p f t", f=8)
        YN = ATY[:].rearrange("p (m e t) -> p m e t", m=2, e=8)

        G1, G2, GF, GCONV, GGM, GLN, BLN, CW = 0, 8, 16, 24, 28, 32, 36, 40

        def pcol(base, k):
            return PRM[:, base + k:base + k + 1]

        state = {"bank": 0, "rs": 0, "ct": 0, "t2": 0, "rl": 0}

        def nxt(name, n):
            v = state[name]
            state[name] = (v + 1) % n
            return v

        def alloc_bank():
            return nxt("bank", 8)

        def pe_group(bank, mms, reads, prio=1):
            def fn(e, mms=mms):
                ins = None
                for (o, l, r, st, sp) in mms:
                    ins = e.matmul(o, lhsT=l, rhs=r, start=st, stop=sp)
                return ins
            dur = 0.03 + sum(max(r.shape[-1], 64) for (_o, _l, r, _a, _b) in mms) / 2300.0
            S.add("pe", fn, reads=reads, writes=[("ps", bank)], dur=dur, prio=prio)

        def wk(slot, qs=(0, 1, 2, 3)):
            return [("W", slot)] + [("Wq", slot, q) for q in qs]

        def cast_piece(i):
            name, sc, src, n = pieces[i]
            S.add("pool", lambda e, sc=sc, src=src: e.dma_start(out=sc, in_=src),
                  writes=[("scr", i)], dma_key=("cast", i), nbytes=6 * 128 * 8 * n)

        def cast_quarter(i, q, dur=None, reads=(), tok=None):
            name, sc, src, n = pieces[i]
            slot = i % 4
            S.add("pool", lambda e: e.dma_start(out=RING[slot][:, :, 256 * q:256 * (q + 1)],
                                                in_=src[:, :, 256 * q:256 * (q + 1)]),
                  reads=list(reads), writes=[("Wq", slot, q)] + ([tok] if tok else []),
                  dma_key=("cast", i, q), nbytes=4 * 128 * 8 * 256, dur=dur)

        def save_piece(i):
            name, sc, src, n = pieces[i]
            slot = i % 4
            S.add("sp", lambda e: e.dma_start(out=sc, in_=RING[slot][:, :, 0:n]),
                  reads=wk(slot), writes=[("scr", i)], dma_key=("wsave", i), nbytes=2 * 128 * 8 * n)

        def cast_piece_to_ring(i, dur=None, reads=(), tok=None):
            name, sc, src, n = pieces[i]
            slot = i % 4
            S.add("pool", lambda e: e.dma_start(out=RING[slot][:, :, 0:n], in_=src),
                  reads=list(reads), writes=wk(slot) + ([tok] if tok else []),
                  dma_key=("cast", i), nbytes=4 * 128 * 8 * n, dur=dur)
            save_piece(i)

        def load_piece(i, g=1):
            if g == 0:
                return cast_piece_to_ring(i)
            name, sc, src, n = pieces[i]
            slot = i % 4
            S.add("sp", lambda e, sc=sc, slot=slot, n=n: e.dma_start(out=RING[slot][:, :, 0:n], in_=sc),
                  reads=[("scr", i)], writes=wk(slot), dma_key=("wl", slot), nbytes=2 * 128 * 8 * n)

        def load_x(m, reads=()):
            xb = m % 3
            for j in range(4):
                src = xT[m, :, 2 * MT * j:2 * MT * (j + 1)].rearrange("p (k t) -> p k t", k=2)
                S.add("sp", lambda e, xb=xb, src=src, j=j: e.dma_start(out=X[xb][:, 2 * j:2 * j + 2, :], in_=src),
                      reads=list(reads), writes=[("X", xb, 2 * j), ("X", xb, 2 * j + 1)], dma_key=("xl", xb, j),
                      nbytes=4 * 128 * 2 * MT)

        def store_x(m, j):
            xb = m % 3
            dst = oT[m, :, 2 * MT * j:2 * MT * (j + 1)].rearrange("p (k t) -> p k t", k=2)
            S.add("sp", lambda e, xb=xb, dst=dst, j=j: e.dma_start(out=dst, in_=X[xb][:, 2 * j:2 * j + 2, :]),
                  reads=[("X", xb, 2 * j), ("X", xb, 2 * j + 1)], writes=[("out", m, j)],
                  dma_key=("xs", xb, j), nbytes=4 * 128 * 2 * MT)

        def rstd_from_ms(out_ap, in_ap, reads, wkey, n):
            S.add("act", lambda e: e.activation(out=out_ap, in_=in_ap, func=AF.Ln, bias=EPST[:, :], scale=1.0),
                  reads=reads + [("const",)], writes=[wkey], n=n, prio=0)
            S.add("act", lambda e: e.activation(out=out_ap, in_=out_ap, func=AF.Exp, scale=-0.5),
                  reads=[wkey], writes=[wkey], n=n, prio=0)

        def rms_stats(src, src_keys, n, ones):
            bank = alloc_bank()
            if n == 8:
                for j in range(4):
                    for c in (2 * j, 2 * j + 1):
                        S.add("act", lambda e, c=c: e.activation(out=SQ[:, c, :], in_=src[:, c, :], func=AF.Square),
                              reads=[src_keys[c]], writes=[("SQ", c)], prio=0)
                    S.add("dve", lambda e, j=j: e.tensor_tensor(out=SQ[:, 2 * j, :], in0=SQ[:, 2 * j, :],
                                                                in1=SQ[:, 2 * j + 1, :], op=ALU.add),
                          reads=[("SQ", 2 * j), ("SQ", 2 * j + 1)], writes=[("SQ", 2 * j)], prio=0, dur=0.42)
                    pe_group(bank, [(PS[bank][:, :], ones[:, :], SQ[:, 2 * j, :], j == 0, j == 3)],
                             reads=[("SQ", 2 * j), ("const",)], prio=0)
            else:
                for c in range(n):
                    S.add("act", lambda e, c=c: e.activation(out=SQ[:, c, :], in_=src[:, c, :], func=AF.Square),
                          reads=[src_keys[c]], writes=[("SQ", c)], prio=0)
                    pe_group(bank, [(PS[bank][:, :], ones[:, :], SQ[:, c, :], c == 0, c == n - 1)],
                             reads=[("SQ", c), ("const",)], prio=0)
            slot = nxt("rs", 3)
            rstd_from_ms(RS[slot][:, :], PS[bank][:, :], [("ps", bank)], ("RS", slot), MT)
            return slot

        def normalize(out_fn, src_fn, gbase, n, slot, in_keys, out_keys, pool_chunks):
            for k in range(n):
                eng = "pool" if k in pool_chunks else "dve"
                S.add(eng, lambda e, k=k: e.scalar_tensor_tensor(
                    out=out_fn(k), in0=src_fn(k), scalar=pcol(gbase, k), in1=RS[slot][:, :],
                    op0=ALU.mult, op1=ALU.mult),
                    reads=[in_keys[k], ("RS", slot), ("const",)], writes=[out_keys[k]], prio=0)

        def stage0(m):
            xb, hm = m % 3, m % 2
            xk = [("X", xb, k) for k in range(8)]
            slot = rms_stats(X[xb], xk, 8, ONESD)
            normalize(lambda k: HT[:, hm, k, :], lambda k: X[xb][:, k, :], G1, 8, slot, xk,
                      [("HT", hm, k) for k in range(8)], ())

        def halo_stage():
            S.add("act", lambda e: e.activation(out=SQH, in_=XH, func=AF.Square),
                  reads=[("YB", 0)], writes=[("VH", 0)])
            bank = alloc_bank()
            pe_group(bank, [(PS[bank][:, 0:HW], ONESD[:, :], SQH[:, k, :], k == 0, k == 7) for k in range(8)],
                     reads=[("VH", 0), ("const",)])
            rstd_from_ms(RSH, PS[bank][:, 0:HW], [("ps", bank)], ("YB", 1), HW)
            for k in range(8):
                S.add("dve", lambda e, k=k: e.scalar_tensor_tensor(
                    out=HH[:, k, :], in0=XH[:, k, :], scalar=pcol(G1, k), in1=RSH,
                    op0=ALU.mult, op1=ALU.mult),
                    reads=[("YB", 0), ("YB", 1), ("const",)], writes=[("VH", 1)])
            bank2 = alloc_bank()
            WBC, WXU = RING[1], RING[2]
            mms = []
            for c in range(4):
                for k in range(8):
                    mms.append((PS[bank2][:, 2 * HW * c:2 * HW * c + HW],
                                WBC[:, k, 512 + c * 128:512 + (c + 1) * 128], HH[:, k, :], k == 0, k == 7))
                for k in range(8):
                    mms.append((PS[bank2][:, 2 * HW * c + HW:2 * HW * (c + 1)],
                                WXU[:, k, c * 128:(c + 1) * 128], HH[:, k, :], k == 0, k == 7))
            pe_group(bank2, mms, reads=[("VH", 1)] + wk(1, (2, 3)) + wk(2, (0, 1)))
            psv = PS[bank2][:, 0:8 * HW].rearrange("p (c j) -> p c j", c=4)
            S.add("act", lambda e: e.activation(out=CH, in_=psv[:, :, 0:HW], func=AF.Identity),
                  reads=[("ps", bank2)], writes=[("YB", 2)])
            S.add("dve", lambda e: e.tensor_tensor(out=ZH, in0=psv[:, :, HW:2 * HW], in1=CH,
                                                   op=ALU.mult),
                  reads=[("ps", bank2), ("YB", 2)], writes=[("YB", 3)])
            S.add("dve", lambda e: e.tensor_copy(out=Z[:, :, 0:2], in_=ZH[:, :, HW - 2:HW]),
                  reads=[("YB", 3)], writes=[("Z", c) for c in range(4)])

        def stage1_v(m):
            hm = m % 2
            WV = RING[0]
            banks = []
            for t in range(4):
                bank = alloc_bank()
                banks.append(bank)
                pe_group(bank, [(PS[bank][:, :], HT[:, hm, k, t * 128:(t + 1) * 128], WV[:, k, 0:512],
                                 k == 0, k == 7) for k in range(8)],
                         reads=[("HT", hm, k) for k in range(8)] + wk(0))
                r1, r2 = nxt("rl", 3), nxt("rl", 3)
                S.add("act", lambda e, t=t, bank=bank, r1=r1: e.activation(
                    out=RL[r1][:, :], in_=PS[bank][:, :], func=AF.Identity, accum_out=LNS[:, 0, t:t + 1]),
                    reads=[("ps", bank)], writes=[("RL", r1), ("LNS", 0, t)], prio=0)
                S.add("act", lambda e, t=t, bank=bank, r2=r2: e.activation(
                    out=RL[r2][:, :], in_=PS[bank][:, :], func=AF.Square, accum_out=LNS[:, 1, t:t + 1]),
                    reads=[("ps", bank)], writes=[("RL", r2), ("LNS", 1, t)], prio=0)
            skeys = [("LNS", a, t) for a in range(2) for t in range(4)]
            mkeys = [("LNM", t) for t in range(4)]
            S.add("dve", lambda e: e.tensor_scalar(out=LNM[:, :, 0], in0=LNS[:, 0, 0:4], scalar1=1.0 / 512.0,
                                                   scalar2=None, op0=ALU.mult),
                  reads=skeys, writes=mkeys, n=4, prio=0)
            S.add("dve", lambda e: e.tensor_tensor(out=LNS[:, 2, 0:4], in0=LNM[:, :, 0], in1=LNM[:, :, 0],
                                                   op=ALU.mult),
                  reads=mkeys, writes=[("LNS", 2)], n=4, prio=0)
            S.add("dve", lambda e: e.scalar_tensor_tensor(out=LNM[:, :, 1], in0=LNS[:, 1, 0:4], scalar=1.0 / 512.0,
                                                          in1=LNS[:, 2, 0:4], op0=ALU.mult, op1=ALU.subtract),
                  reads=skeys + [("LNS", 2)], writes=mkeys, n=4, prio=0)
            rstd_from_ms(LNR[:, :], LNM[:, :, 1], mkeys, ("LNR",), 4)
            S.add("dve", lambda e: e.scalar_tensor_tensor(out=LNB[:, :], in0=LNM[:, :, 0], scalar=-1.0,
                                                          in1=LNR[:, :], op0=ALU.mult, op1=ALU.mult),
                  reads=[("LNR",)] + [("LNM", t) for t in range(4)], writes=[("LNB",)])
            for t in range(4):
                bank = banks[t]
                S.add("act", lambda e, t=t, bank=bank: e.activation(
                    out=VH[:, t, :], in_=PS[bank][:, :], func=AF.Identity,
                    bias=LNB[:, t:t + 1], scale=LNR[:, t:t + 1]),
                    reads=[("ps", bank), ("LNR",), ("LNB",)], writes=[("VH", t)])

        def stage1_conv(m):
            hm = m % 2
            WBC, WXU = RING[1], RING[2]
            hreads = [("HT", hm, k) for k in range(8)]

            def gate_b(c):
                bB = alloc_bank()
                pe_group(bB, [(PS[bB][:, :], WBC[:, k, c * 128:(c + 1) * 128], HT[:, hm, k, :],
                               k == 0, k == 7) for k in range(8)], reads=hreads + wk(1, (c // 2,)))
                S.add("dve", lambda e, c=c, bB=bB: e.tensor_tensor(
                    out=YA[:, c, :], in0=PS[bB][:, :], in1=YA[:, c, :], op=ALU.mult),
                    reads=[("ps", bB), ("YA", c)], writes=[("YA", c)])

            for c in range(4):
                bC, bX = alloc_bank(), alloc_bank()
                pe_group(bC, [(PS[bC][:, :], WBC[:, k, 512 + c * 128:512 + (c + 1) * 128], HT[:, hm, k, :],
                               k == 0, k == 7) for k in range(8)], reads=hreads + wk(1, (2 + c // 2,)))
                pe_group(bX, [(PS[bX][:, :], WXU[:, k, c * 128:(c + 1) * 128], HT[:, hm, k, :],
                               k == 0, k == 7) for k in range(8)], reads=hreads + wk(2, (c // 2,)))
                if c > 0:
                    gate_b(c - 1)
                ci = nxt("ct", 2)
                S.add("act", lambda e, ci=ci, bC=bC: e.activation(out=CT[ci][:, :], in_=PS[bC][:, :], func=AF.Identity),
                      reads=[("ps", bC)], writes=[("CT", ci)])
                S.add("dve", lambda e, c=c, ci=ci, bX=bX: e.tensor_tensor(
                    out=Z[:, c, 2:2 + MT], in0=PS[bX][:, :], in1=CT[ci][:, :], op=ALU.mult),
                    reads=[("ps", bX), ("CT", ci)], writes=[("Z", c)])
                S.add("act", lambda e, c=c: e.activation(out=YA[:, c, :], in_=Z[:, c, 2:2 + MT], func=AF.Identity,
                                                         scale=pcol(CW, 3 * c + 2)),
                      reads=[("Z", c), ("const",)], writes=[("YA", c)])
                S.add("dve", lambda e, c=c: e.scalar_tensor_tensor(
                    out=YA[:, c, :], in0=Z[:, c, 1:1 + MT], scalar=pcol(CW, 3 * c + 1), in1=YA[:, c, :],
                    op0=ALU.mult, op1=ALU.add),
                    reads=[("Z", c), ("YA", c), ("const",)], writes=[("YA", c)])
                S.add("dve", lambda e, c=c: e.scalar_tensor_tensor(
                    out=YA[:, c, :], in0=Z[:, c, 0:MT], scalar=pcol(CW, 3 * c + 0), in1=YA[:, c, :],
                    op0=ALU.mult, op1=ALU.add),
                    reads=[("Z", c), ("YA", c), ("const",)], writes=[("YA", c)])
                S.add("pool", lambda e, c=c: e.tensor_copy(out=Z[:, c, 0:2], in_=Z[:, c, MT:MT + 2]),
                      reads=[("Z", c)], writes=[("Z", c)], dur=0.3)
            gate_b(3)

        def stage1_su(m):
            hm = m % 2
            WXU = RING[2]
            hreads = [("HT", hm, k) for k in range(8)]
            for c in range(4):
                bS, bU = alloc_bank(), alloc_bank()
                mms = []
                for t in range(4):
                    for hh in range(2):
                        h = 2 * c + hh
                        mms.append((PS[bS][64 * hh:64 * hh + 64, t * 128:(t + 1) * 128],
                                    VH[:, t, h * 64:(h + 1) * 64], WST[:, h, :], True, True))
                pe_group(bS, mms, reads=[("VH", t) for t in range(4)] + [("WST",)])
                pe_group(bU, [(PS[bU][:, :], WXU[:, k, 512 + c * 128:512 + (c + 1) * 128], HT[:, hm, k, :],
                               k == 0, k == 7) for k in range(8)], reads=hreads + wk(2, (2 + c // 2,)))
                ti = nxt("t2", 2)
                bias_b = BIAS[:, c, :].unsqueeze(1).broadcast_to([128, 4, 128])
                S.add("dve", lambda e, c=c, ti=ti, bS=bS, bias_b=bias_b: e.scalar_tensor_tensor(
                    out=T2[ti][:, :].rearrange("p (t i) -> p t i", t=4),
                    in0=PS[bS][:, :].rearrange("p (t i) -> p t i", t=4),
                    scalar=pcol(GLN, c), in1=bias_b, op0=ALU.mult, op1=ALU.add),
                    reads=[("ps", bS), ("BIAS",), ("const",)], writes=[("T2", ti)])
                S.add("dve", lambda e, c=c, ti=ti, bU=bU: e.tensor_tensor(
                    out=YB[:, c, :], in0=PS[bU][:, :], in1=T2[ti][:, :], op=ALU.mult),
                    reads=[("ps", bU), ("T2", ti)], writes=[("YB", c)])

        def y_norm(m, which):
            mm_ = m % 2
            if which == "a":
                src, key, gbase, ebase = YA, "YA", GCONV, 0
            else:
                src, key, gbase, ebase = YB, "YB", GGM, 4
            yk = [(key, c) for c in range(4)]
            slot = rms_stats(src, yk, 4, ONESE)
            normalize(lambda c: YN[:, mm_, ebase + c, :], lambda c: src[:, c, :], gbase, 4, slot, yk,
                      [("ATY", mm_ * 8 + ebase + c) for c in range(4)], ())

        def out_proj(m):
            xb, mm_ = m % 3, m % 2
            WO = RING[3]
            for dc in range(8):
                bank = alloc_bank()
                pe_group(bank, [(PS[bank][:, :], WO[:, e_, dc * 128:(dc + 1) * 128], YN[:, mm_, e_, :],
                                 e_ == 0, e_ == 7) for e_ in range(8)],
                         reads=[("ATY", mm_ * 8 + e_) for e_ in range(8)] + wk(3))
                S.add("dve", lambda e, dc=dc, bank=bank: e.tensor_tensor(
                    out=X[xb][:, dc, :], in0=PS[bank][:, :], in1=X[xb][:, dc, :], op=ALU.add),
                    reads=[("ps", bank), ("X", xb, dc)], writes=[("X", xb, dc)])

        def norm2(m):
            xb, hm = m % 3, m % 2
            xk = [("X", xb, k) for k in range(8)]
            slot = rms_stats(X[xb], xk, 8, ONESD)
            normalize(lambda k: HT[:, hm, k, :], lambda k: X[xb][:, k, :], G2, 8, slot, xk,
                      [("HT", hm, k) for k in range(8)], ())

        def ffn_up(m, p):
            hm = m % 2
            slot = (4 + 2 * p) % 4
            WU = RING[slot]
            hreads = [("HT", hm, k) for k in range(8)]
            for fc in range(8):
                bank = alloc_bank()
                pe_group(bank, [(PS[bank][:, :], WU[:, k, fc * 128:(fc + 1) * 128], HT[:, hm, k, :],
                                 k == 0, k == 7) for k in range(8)], reads=hreads + wk(slot))
                ri = nxt("rl", 3)
                S.add("act", lambda e, ri=ri, bank=bank: e.activation(out=RL[ri][:, :], in_=PS[bank][:, :],
                                                                      func=AF.Relu),
                      reads=[("ps", bank)], writes=[("RL", ri)])
                S.add("act", lambda e, ri=ri, fc=fc: e.activation(
                    out=AT[:, fc, hm * MT:(hm + 1) * MT], in_=RL[ri][:, :], func=AF.Square),
                    reads=[("RL", ri)], writes=[("ATY", fc * 2 + hm)])

        def ffn_down(m, p):
            xb, hm = m % 3, m % 2
            slot = (5 + 2 * p) % 4
            WD = RING[slot]
            for dc in range(8):
                bank = alloc_bank()
                pe_group(bank, [(PS[bank][:, :], WD[:, fc, dc * 128:(dc + 1) * 128],
                                 AT[:, fc, hm * MT:(hm + 1) * MT], fc == 0, fc == 7) for fc in range(8)],
                         reads=[("ATY", fc * 2 + hm) for fc in range(8)] + wk(slot))
                S.add("dve", lambda e, dc=dc, bank=bank: e.tensor_tensor(
                    out=X[xb][:, dc, :], in0=PS[bank][:, :], in1=X[xb][:, dc, :], op=ALU.add),
                    reads=[("ps", bank), ("X", xb, dc)], writes=[("X", xb, dc)])

        def final(m):
            xb = m % 3
            xk = [("X", xb, k) for k in range(8)]
            slot = rms_stats(X[xb], xk, 8, ONESD)
            normalize(lambda k: X[xb][:, k, :], lambda k: X[xb][:, k, :], GF, 8, slot, xk, xk, ())
            for j in range(4):
                store_x(m, j)

        wv = 3.0 + 2.1e6 / 160e3
        tokv = [("tok", 9)]
        cast_piece_to_ring(0, dur=wv, tok=tokv[0])
        w0 = 3.0 + 3.2e6 / 200e3
        toks = [("tok", j) for j in range(3)]
        for j, (i, q) in enumerate(((2, 0), (1, 2), (1, 0))):
            cast_quarter(i, q, dur=w0, reads=tokv, tok=toks[j])
        w1 = 3.0 + 3.2e6 / 200e3
        toks1 = [("tok", 4 + j) for j in range(3)]
        for j, (i, q) in enumerate(((2, 1), (1, 3), (1, 1))):
            cast_quarter(i, q, dur=w1, reads=toks, tok=toks1[j])
        w2 = 3.0 + 6.3e6 / 220e3
        for (i, q) in ((2, 2), (2, 3)):
            cast_quarter(i, q, dur=w2, reads=toks1)
        save_piece(1)
        save_piece(2)
        cast_piece_to_ring(3, dur=w2, reads=toks1)
        S.add("sp", lambda e: e.dma_start(out=PRM[:, :], in_=prm_d), writes=[("prm",)], dma_key=("misc", 0), prio=0)
        S.add("sp", lambda e: e.dma_start(out=BSF[:, :, :], in_=bsf_d), writes=[("bsf",)], dma_key=("misc", 1), prio=0)
        WSF = [RL[0][:, :].rearrange("p (h i) -> p h i", h=4), RL[1][:, :].rearrange("p (h i) -> p h i", h=4)]
        S.add("sp", lambda e: e.dma_start(out=WSF[0], in_=wst_d[:, 0:4, :]), writes=[("RL", 0)], dma_key=("misc", 2), prio=0)
        S.add("sp", lambda e: e.dma_start(out=WSF[1], in_=wst_d[:, 4:8, :]), writes=[("RL", 1)], dma_key=("misc", 3), prio=0)
        S.add("sp", lambda e: e.dma_start(out=XH, in_=xh), writes=[("YB", 0)], dma_key=("misc", 4), prio=0)
        load_x(0)
        load_x(1, reads=toks1)

        def consts(e):
            e.memset(ONESD[:, :], 1.0 / D)
            e.memset(ONESE[:, :], 1.0 / 512.0)
            e.memset(ONE1[:, :], 1.0)
            return e.memset(EPST[:, :], EPS)
        S.add("dve", consts, writes=[("const0",)])
        S.add("dve", lambda e: e.tensor_copy(out=WST[:, 0:4, :], in_=WSF[0]),
              reads=[("RL", 0), ("const0",), ("prm",)], writes=[("WST",), ("const",)])
        S.add("dve", lambda e: e.tensor_copy(out=WST[:, 4:8, :], in_=WSF[1]),
              reads=[("RL", 1)], writes=[("WST",)])
        S.add("dve", lambda e: e.memset(WST[64:128, :, 0:64], 0.0), reads=[("WST",)], writes=[("WST",)])
        bank = alloc_bank()
        mms = []
        for c in range(4):
            for hh in range(2):
                mms.append((PS[bank][64 * hh:64 * hh + 64, c * 128:(c + 1) * 128], ONE1[:, :],
                            WST[:, 2 * c + hh, :], True, True))
        pe_group(bank, mms, reads=[("WST",), ("const",)])
        S.add("dve", lambda e: e.tensor_tensor(out=BIAS[:, :, :],
                                               in0=PS[bank][:, :].rearrange("p (c i) -> p c i", c=4),
                                               in1=PRM[:, BLN:BLN + 4].unsqueeze(2).broadcast_to([128, 4, 128]),
                                               op=ALU.mult),
              reads=[("ps", bank), ("const",)], writes=[("BIAS",)])
        S.add("dve", lambda e: e.tensor_tensor(out=BIAS[:, :, :], in0=BIAS[:, :, :], in1=BSF[:, :, :], op=ALU.add),
              reads=[("BIAS",), ("bsf",)], writes=[("BIAS",)])

        stage0(0)
        halo_stage()
        if debug:
            S.add("sp", lambda e: e.dma_start(out=dbg_z, in_=Z[:, :, 0:2]), reads=[("Z", c) for c in range(4)],
                  writes=[("dbg", 0)], dma_key=("dbg", 0))
            S.add("sp", lambda e: e.dma_start(out=dbg_ch, in_=CH), reads=[("YB", 2)],
                  writes=[("dbg", 1)], dma_key=("dbg", 1))
            S.add("sp", lambda e: e.dma_start(out=dbg_rsh, in_=RSH), reads=[("YB", 1)],
                  writes=[("dbg", 2)], dma_key=("dbg", 2))
            S.add("sp", lambda e: e.dma_start(out=dbg_hh, in_=HH), reads=[("VH", 1)],
                  writes=[("dbg", 3)], dma_key=("dbg", 3))

        for g in range(NGRP):
            mA, mB = 2 * g, 2 * g + 1
            last = g == NGRP - 1
            stage1_v(mA)
            stage1_conv(mA)
            stage1_su(mA)
            stage0(mB)
            stage1_v(mB)
            load_piece(4, g)
            y_norm(mA, "a")
            stage1_conv(mB)
            load_piece(5, g)
            y_norm(mA, "b")
            stage1_su(mB)
            load_piece(6, g)
            out_proj(mA)
            y_norm(mB, "a")
            y_norm(mB, "b")
            norm2(mA)
            out_proj(mB)
            load_piece(7, g)
            if not last:
                load_x(mA + 2)
            norm2(mB)
            for p in range(4):
                ffn_up(mA, p)
                ffn_up(mB, p)
                idx = 8 + 2 * p
                if idx < NPIECE:
                    load_piece(idx, g)
                elif not last:
                    load_piece(idx - NPIECE)
                if p == 3 and not last:
                    stage0(mA + 2)
                ffn_down(mA, p)
                ffn_down(mB, p)
                idx = 9 + 2 * p
                if idx < NPIECE:
                    load_piece(idx, g)
                elif not last:
                    load_piece(idx - NPIECE)
            final(mA)
            final(mB)
            if not last:
                load_x(mB + 2)

        S.schedule()
        dma_keys = S.resolve()
        eng_sems = {e_: es.enter_context(nc.semaphore(f"sem_{e_}")) for e_ in ENGS}
        dma_sems = {k: es.enter_context(nc.semaphore("dma_" + "_".join(str(x) for x in k))) for k in dma_keys}
        S.assign(eng_sems, dma_sems)

        block = es.enter_context(nc.Block())

        @block.tensor
        def _(e):
            S.emit("pe", e)

        @block.scalar
        def _(e):
            S.emit("act", e)

        @block.vector
        def _(e):
            S.emit("dve", e)

        @block.gpsimd
        def _(e):
            S.emit("pool", e)

        @block.sync
        def _(e):
            S.emit("sp", e)
            for k in dma_sems:
                if k[0] == "xs":
                    e.wait_ge(dma_sems[k], S.final_dma[k])
    return nc


_CACHE = {}


def _host_inputs(inp):
    f = lambda a: np.ascontiguousarray(np.asarray(a, dtype=np.float32))
    x = f(inp["x"])
    prm = np.zeros((128, NPRM), np.float32)

    def cols(v, n):
        return f(v).reshape(n, 128).T

    prm[:, 0:8] = cols(inp["norm1_g"][0], 8)
    prm[:, 8:16] = cols(inp["norm2_g"][0], 8)
    prm[:, 16:24] = cols(inp["final_g"], 8)
    prm[:, 24:28] = cols(inp["out_norm_conv_g"][0], 4)
    prm[:, 28:32] = cols(inp["out_norm_gmlp_g"][0], 4)
    prm[:, 32:36] = cols(inp["gmlp_ln_g"][0], 4)
    prm[:, 36:40] = cols(inp["gmlp_ln_b"][0], 4)
    cw = f(inp["conv_w"][0]).reshape(4, 128, 3).transpose(1, 0, 2).reshape(128, 12)
    prm[:, 40:52] = cw
    bs = f(inp["gmlp_bs"][0])
    bsf = np.repeat(bs.reshape(4, 2, 1, 128), 64, axis=2).reshape(4, 128, 128).transpose(1, 0, 2)
    wst = f(inp["gmlp_ws"][0]).transpose(2, 0, 1)
    shared = {
        "prm": np.ascontiguousarray(prm),
        "bsf": np.ascontiguousarray(bsf),
        "wst": np.ascontiguousarray(wst),
        "w_in": f(inp["w_in"][0]),
        "w_out": f(inp["w_out"][0]),
        "w_up": f(inp["w_up"][0]),
        "w_down": f(inp["w_down"][0]),
    }
    maps = []
    for c in range(NCORES):
        b, half = c // 2, c % 2
        t0 = half * TOK
        xc = x[b, t0:t0 + TOK, :]
        xh = np.zeros((HW, D), np.float32)
        if half == 1:
            xh[HW - 2:HW] = x[b, t0 - 2:t0, :]
        m = dict(shared)
        m["xT"] = np.ascontiguousarray(xc.reshape(NMT, MT, 8, 128).transpose(0, 3, 2, 1)).reshape(NMT, 128, 8 * MT)
        m["xh"] = np.ascontiguousarray(xh.T.reshape(8, 128, HW).transpose(1, 0, 2))
        maps.append(m)
    return maps


def kernel(**inputs):
    if "nc" not in _CACHE:
        _CACHE["nc"] = build_program()
    nc = _CACHE["nc"]
    maps = _host_inputs(inputs)
    res = run_bass_kernel_spmd(nc, maps, core_ids=list(range(NCORES)))
    out = np.empty((BATCH, SEQ, D), np.float32)
    for c in range(NCORES):
        b, half = c // 2, c % 2
        o = np.asarray(res.results[c]["oT"]).reshape(NMT, 128, 8, MT)
        out[b, half * TOK:(half + 1) * TOK, :] = o.transpose(0, 3, 2, 1).reshape(TOK, D)
    return out
```

```python
import contextlib
import numpy as np
import concourse.bass as bass
import concourse.mybir as mybir
from concourse.bass_utils import run_bass_kernel_spmd

F32 = mybir.dt.float32
BF16 = mybir.dt.bfloat16
ALU = mybir.AluOpType
AF = mybir.ActivationFunctionType

NCORES = 8
D = 1024
SEQ = 8192
BATCH = 4
TOK = SEQ * BATCH // NCORES
MT = 512
NMT = TOK // MT
NGRP = NMT // 2
EPS = 1e-6
NPRM = 52
HW = 32

ENGS = ("pe", "act", "dve", "pool", "sp")


class Op:
    __slots__ = ("eng", "fn", "dma_key", "deps", "needs_inc", "sem", "semval", "waits",
                 "idx", "dur", "preds", "succs", "npred", "ready_t", "start", "finish", "bl", "prio")

    def __init__(self, eng, fn, dma_key, idx, dur):
        self.eng = eng
        self.fn = fn
        self.dma_key = dma_key
        self.idx = idx
        self.dur = dur
        self.deps = []
        self.needs_inc = False
        self.sem = None
        self.semval = None
        self.waits = []


class Sched:
    def __init__(self):
        self.ops = []
        self.by_eng = {e: [] for e in ENGS}
        self.last_w = {}
        self.readers = {}
        self.two_pass = False
        self.event_order = False

    def add(self, eng, fn, reads=(), writes=(), dma_key=None, dur=None, n=512, nbytes=0, prio=1):
        if dur is None:
            if dma_key is not None:
                dur = 4.0 + nbytes / 80e3
            elif eng == "act":
                dur = (n + 335) / 1200.0
            elif eng == "dve":
                dur = (n + 151) / 960.0
            elif eng == "pool":
                dur = (2.2 * n + 160) / 1000.0
            else:
                dur = 0.25
        op = Op(eng, fn, dma_key, len(self.ops), dur)
        op.prio = prio
        for k in reads:
            w = self.last_w.get(k)
            if w is not None:
                op.deps.append((w, "RAW"))
        for k in writes:
            w = self.last_w.get(k)
            if w is not None:
                op.deps.append((w, "WAW"))
            for r in self.readers.get(k, ()):
                op.deps.append((r, "WAR"))
        for k in reads:
            self.readers.setdefault(k, []).append(op)
        for k in writes:
            self.last_w[k] = op
            self.readers[k] = []
        self.ops.append(op)
        self.by_eng[eng].append(op)
        return op

    def schedule(self, lat=0.2):
        ops = self.ops
        for op in ops:
            preds = {}
            for d, _kind in op.deps:
                if d is not op:
                    preds[id(d)] = d
            op.preds = list(preds.values())
            op.succs = []
        for op in ops:
            for d in op.preds:
                d.succs.append(op)
        for op in reversed(ops):
            b = 0.0
            for sc in op.succs:
                if sc.bl + lat > b:
                    b = sc.bl + lat
            op.bl = b + op.dur

        def run(keyfn, window):
            for op in ops:
                op.npred = len(op.preds)
                op.ready_t = 0.0
            ready = {e: [] for e in ENGS}
            for op in ops:
                if op.npred == 0:
                    ready[op.eng].append(op)
            t_free = {e: 0.0 for e in ENGS}
            order = []
            while True:
                best = None
                for e in ENGS:
                    lst = ready[e]
                    if not lst:
                        continue
                    tf = t_free[e]
                    horizon = max(tf, min(o.ready_t for o in lst)) + window
                    cand = None
                    for op in lst:
                        if op.ready_t <= horizon:
                            k = keyfn(op)
                            if cand is None or k < cand[0]:
                                cand = (k, op)
                    op = cand[1]
                    st = op.ready_t if op.ready_t > tf else tf
                    if best is None or (st, op.idx) < best[0]:
                        best = ((st, op.idx), op)
                if best is None:
                    break
                (st, _), op = best
                ready[op.eng].remove(op)
                op.start = st
                op.finish = st + op.dur
                t_free[op.eng] = st + (0.1 if op.dma_key is not None else op.dur)
                order.append(op)
                for sc in op.succs:
                    sc.npred -= 1
                    rt = op.finish + lat
                    if rt > sc.ready_t:
                        sc.ready_t = rt
                    if sc.npred == 0:
                        ready[sc.eng].append(sc)
            assert len(order) == len(ops)
            return order

        order = run(lambda op: (op.prio, -op.bl, op.idx), 0.0)
        if self.two_pass:
            total = max(o.finish for o in order)
            ls = {}
            for op in reversed(order):
                v = total
                for sc in op.succs:
                    c = ls[id(sc)] - lat
                    if c < v:
                        v = c
                v -= op.dur
                if op.eng == "pe" and op.start < v:
                    v = op.start
                ls[id(op)] = v
            order = run(lambda op: (ls[id(op)], op.idx), 0.4)
        if self.event_order:
            ev = {}
            for op in order:
                v = 0.0
                for p in op.preds:
                    c = p.finish if (p.eng == "pe" or p.dma_key is not None) else ev[id(p)]
                    if c > v:
                        v = c
                ev[id(op)] = v
            key = {}
            for op in order:
                if op.eng == "pe":
                    key[id(op)] = (op.start, op.start, op.idx)
                else:
                    key[id(op)] = (ev[id(op)], op.start, op.idx)
            order = sorted(order, key=lambda o: key[id(o)])
            by_eng = {e: [o for o in order if o.eng == e] for e in ENGS}
            pos = {e: 0 for e in ENGS}
            t_free = {e: 0.0 for e in ENGS}
            done = set()
            n_done = 0
            while n_done < len(order):
                progressed = False
                for e in ENGS:
                    while pos[e] < len(by_eng[e]):
                        op = by_eng[e][pos[e]]
                        if any(id(p) not in done for p in op.preds):
                            break
                        st = t_free[e]
                        for p in op.preds:
                            if p.finish + lat > st:
                                st = p.finish + lat
                        op.start = st
                        op.finish = st + op.dur
                        t_free[e] = st + (0.1 if op.dma_key is not None else op.dur)
                        done.add(id(op))
                        pos[e] += 1
                        n_done += 1
                        progressed = True
                assert progressed, "engine queues deadlock"
        self.ops = order
        self.by_eng = {e: [o for o in order if o.eng == e] for e in ENGS}
        self.est_total = max(o.finish for o in order)

    def resolve(self):
        for op in self.ops:
            need = {}
            for d, kind in op.deps:
                if d is op:
                    continue
                if d.dma_key is None and d.eng == op.eng and op.eng == "pe":
                    continue
                need[id(d)] = d
            op.waits = list(need.values())
            for d in op.waits:
                d.needs_inc = True
        dma_keys = []
        for op in self.ops:
            if op.dma_key is not None:
                op.needs_inc = True
                if op.dma_key not in dma_keys:
                    dma_keys.append(op.dma_key)
        return dma_keys

    def assign(self, eng_sems, dma_sems):
        cnt = {e: 0 for e in ENGS}
        dcnt = {}
        for op in self.ops:
            if op.dma_key is not None:
                dcnt[op.dma_key] = dcnt.get(op.dma_key, 0) + 16
                op.sem = dma_sems[op.dma_key]
                op.semval = dcnt[op.dma_key]
            elif op.needs_inc:
                cnt[op.eng] += 1
                op.sem = eng_sems[op.eng]
                op.semval = cnt[op.eng]
        self.final_dma = dict(dcnt)

    def emit(self, eng, e):
        waited = {}
        for op in self.by_eng[eng]:
            best = {}
            for d in op.waits:
                key = id(d.sem)
                if d.semval > best.get(key, (None, 0))[1]:
                    best[key] = (d.sem, d.semval)
            for key, (sem, val) in best.items():
                if waited.get(key, 0) >= val:
                    continue
                e.wait_ge(sem, val)
                waited[key] = val
            ins = op.fn(e)
            if op.dma_key is not None:
                ins.then_inc(op.sem, 16)
            elif op.needs_inc:
                ins.then_inc(op.sem, 1)


def build_program(debug=False):
    nc = bass.Bass("TRN2", target_bir_lowering=False)
    S = Sched()
    if debug:
        dbg_z = nc.dram_tensor("dbg_z", [128, 4, 2], F32, kind="ExternalOutput").ap()
        dbg_ch = nc.dram_tensor("dbg_ch", [128, 4, HW], F32, kind="ExternalOutput").ap()
        dbg_rsh = nc.dram_tensor("dbg_rsh", [128, HW], F32, kind="ExternalOutput").ap()
        dbg_hh = nc.dram_tensor("dbg_hh", [128, 8, HW], BF16, kind="ExternalOutput").ap()

    xT = nc.dram_tensor("xT", [NMT, 128, 8 * MT], F32, kind="ExternalInput").ap()
    xh = nc.dram_tensor("xh", [128, 8, HW], F32, kind="ExternalInput").ap()
    prm_d = nc.dram_tensor("prm", [128, NPRM], F32, kind="ExternalInput").ap()
    bsf_d = nc.dram_tensor("bsf", [128, 4, 128], F32, kind="ExternalInput").ap()
    wst_d = nc.dram_tensor("wst", [128, 8, 128], F32, kind="ExternalInput").ap()
    w_in = nc.dram_tensor("w_in", [D, 2560], F32, kind="ExternalInput").ap()
    w_out = nc.dram_tensor("w_out", [D, D], F32, kind="ExternalInput").ap()
    w_up = nc.dram_tensor("w_up", [D, 4096], F32, kind="ExternalInput").ap()
    w_down = nc.dram_tensor("w_down", [4096, D], F32, kind="ExternalInput").ap()
    oT = nc.dram_tensor("oT", [NMT, 128, 8 * MT], F32, kind="ExternalOutput").ap()

    def scr(name, n):
        return nc.dram_tensor(name, [128, 8, n], BF16, kind="Internal").ap()

    pieces = []
    pieces.append(("V", scr("sc_v", 512), w_in[:, 2048:2560].rearrange("(k p) n -> p k n", p=128), 512))
    pieces.append(("BC", scr("sc_bc", 1024), w_in[:, 0:1024].rearrange("(k p) n -> p k n", p=128), 1024))
    pieces.append(("XU", scr("sc_xu", 1024), w_in[:, 1024:2048].rearrange("(k p) n -> p k n", p=128), 1024))
    pieces.append(("OUT", scr("sc_out", 1024), w_out.rearrange("(k p) n -> p k n", p=128), 1024))
    for p in range(4):
        pieces.append((f"UP{p}", scr(f"sc_up{p}", 1024),
                       w_up[:, 1024 * p:1024 * (p + 1)].rearrange("(k p) n -> p k n", p=128), 1024))
        pieces.append((f"DN{p}", scr(f"sc_dn{p}", 1024),
                       w_down[1024 * p:1024 * (p + 1), :].rearrange("(k p) n -> p k n", p=128), 1024))
    NPIECE = len(pieces)

    es = contextlib.ExitStack()
    with es:
        def sb(name, shape, dt):
            return es.enter_context(nc.sbuf_tensor(name, shape, dt))

        X = [sb(f"X{i}", [128, 8, MT], F32) for i in range(3)]
        HT = sb("HT", [128, 2, 8, MT], BF16)
        SQ = sb("SQ", [128, 8, MT], BF16)
        RS = [sb(f"RS{i}", [128, MT], F32) for i in range(3)]
        Z = sb("Z", [128, 4, MT + 4], F32)
        CT = [sb(f"CT{i}", [128, MT], F32) for i in range(2)]
        YA = sb("YA", [128, 4, MT], F32)
        T2 = [sb(f"T2{i}", [128, MT], F32) for i in range(2)]
        YB = sb("YB", [128, 4, MT], F32)
        VH = sb("VH", [128, 4, MT], BF16)
        ATY = sb("ATY", [128, 16 * MT], BF16)
        RL = [sb(f"RL{i}", [128, MT], F32) for i in range(3)]
        RING = [sb(f"RING{i}", [128, 8, 1024], BF16) for i in range(4)]
        PRM = sb("PRM", [128, NPRM], F32)
        BSF = sb("BSF", [128, 4, 128], F32)
        BIAS = sb("BIAS", [128, 4, 128], F32)
        WST = sb("WST", [128, 8, 128], BF16)
        ONESD = sb("ONESD", [128, 128], BF16)
        ONESE = sb("ONESE", [128, 128], BF16)
        ONE1 = sb("ONE1", [128, 64], BF16)
        EPST = sb("EPST", [128, 1], F32)
        LNS = sb("LNS", [128, 4, 6], F32)
        LNM = sb("LNM", [128, 4, 2], F32)
        LNR = sb("LNR", [128, 4], F32)
        LNB = sb("LNB", [128, 4], F32)
        PS = [es.enter_context(nc.psum_tensor(f"PS{i}", [128, MT], F32)) for i in range(8)]

        XH = YB[:, 0, 0:8 * HW].rearrange("p (k j) -> p k j", k=8)
        RSH = YB[:, 1, 0:HW]
        CH = YB[:, 2, 0:4 * HW].rearrange("p (c j) -> p c j", c=4)
        ZH = YB[:, 3, 0:4 * HW].rearrange("p (c j) -> p c j", c=4)
        SQH = VH[:, 0, 0:8 * HW].rearrange("p (k j) -> p k j", k=8)
        HH = VH[:, 1, 0:8 * HW].rearrange("p (k j) -> p k j", k=8)
        AT = ATY[:].rearrange("p (f t) -> p f t", f=8)
        YN = ATY[:].rearrange("p (m e t) -> p m e t", m=2, e=8)

        G1, G2, GF, GCONV, GGM, GLN, BLN, CW = 0, 8, 16, 24, 28, 32, 36, 40

        def pcol(base, k):
            return PRM[:, base + k:base + k + 1]

        state = {"bank": 0, "rs": 0, "ct": 0, "t2": 0, "rl": 0}

        def nxt(name, n):
            v = state[name]
            state[name] = (v + 1) % n
            return v

        def alloc_bank():
            return nxt("bank", 8)

        def pe_group(bank, mms, reads, prio=1):
            def fn(e, mms=mms):
                ins = None
                for (o, l, r, st, sp) in mms:
                    ins = e.matmul(o, lhsT=l, rhs=r, start=st, stop=sp)
                return ins
            dur = 0.03 + sum(max(r.shape[-1], 64) for (_o, _l, r, _a, _b) in mms) / 2300.0
            S.add("pe", fn, reads=reads, writes=[("ps", bank)], dur=dur, prio=prio)

        def wk(slot, qs=(0, 1, 2, 3)):
            return [("W", slot)] + [("Wq", slot, q) for q in qs]

        def cast_piece(i):
            name, sc, src, n = pieces[i]
            S.add("pool", lambda e, sc=sc, src=src: e.dma_start(out=sc, in_=src),
                  writes=[("scr", i)], dma_key=("cast", i), nbytes=6 * 128 * 8 * n)

        def cast_quarter(i, q, dur=None, reads=(), tok=None):
            name, sc, src, n = pieces[i]
            slot = i % 4
            S.add("pool", lambda e: e.dma_start(out=RING[slot][:, :, 256 * q:256 * (q + 1)],
                                                in_=src[:, :, 256 * q:256 * (q + 1)]),
                  reads=list(reads), writes=[("Wq", slot, q)] + ([tok] if tok else []),
                  dma_key=("cast", i, q), nbytes=4 * 128 * 8 * 256, dur=dur)

        def save_piece(i):
            name, sc, src, n = pieces[i]
            slot = i % 4
            S.add("sp", lambda e: e.dma_start(out=sc, in_=RING[slot][:, :, 0:n]),
                  reads=wk(slot), writes=[("scr", i)], dma_key=("wsave", i), nbytes=2 * 128 * 8 * n)

        def cast_piece_to_ring(i, dur=None, reads=(), tok=None):
            name, sc, src, n = pieces[i]
            slot = i % 4
            S.add("pool", lambda e: e.dma_start(out=RING[slot][:, :, 0:n], in_=src),
                  reads=list(reads), writes=wk(slot) + ([tok] if tok else []),
                  dma_key=("cast", i), nbytes=4 * 128 * 8 * n, dur=dur)
            save_piece(i)

        def load_piece(i, g=1):
            if g == 0:
                return cast_piece_to_ring(i)
            name, sc, src, n = pieces[i]
            slot = i % 4
            S.add("sp", lambda e, sc=sc, slot=slot, n=n: e.dma_start(out=RING[slot][:, :, 0:n], in_=sc),
                  reads=[("scr", i)], writes=wk(slot), dma_key=("wl", slot), nbytes=2 * 128 * 8 * n)

        def load_x(m, reads=()):
            xb = m % 3
            for j in range(4):
                src = xT[m, :, 2 * MT * j:2 * MT * (j + 1)].rearrange("p (k t) -> p k t", k=2)
                S.add("sp", lambda e, xb=xb, src=src, j=j: e.dma_start(out=X[xb][:, 2 * j:2 * j + 2, :], in_=src),
                      reads=list(reads), writes=[("X", xb, 2 * j), ("X", xb, 2 * j + 1)], dma_key=("xl", xb, j),
                      nbytes=4 * 128 * 2 * MT)

        def store_x(m, j):
            xb = m % 3
            dst = oT[m, :, 2 * MT * j:2 * MT * (j + 1)].rearrange("p (k t) -> p k t", k=2)
            S.add("sp", lambda e, xb=xb, dst=dst, j=j: e.dma_start(out=dst, in_=X[xb][:, 2 * j:2 * j + 2, :]),
                  reads=[("X", xb, 2 * j), ("X", xb, 2 * j + 1)], writes=[("out", m, j)],
                  dma_key=("xs", xb, j), nbytes=4 * 128 * 2 * MT)

        def rstd_from_ms(out_ap, in_ap, reads, wkey, n):
            S.add("act", lambda e: e.activation(out=out_ap, in_=in_ap, func=AF.Ln, bias=EPST[:, :], scale=1.0),
                  reads=reads + [("const",)], writes=[wkey], n=n, prio=0)
            S.add("act", lambda e: e.activation(out=out_ap, in_=out_ap, func=AF.Exp, scale=-0.5),
                  reads=[wkey], writes=[wkey], n=n, prio=0)

        def rms_stats(src, src_keys, n, ones):
            bank = alloc_bank()
            if n == 8:
                for j in range(4):
                    for c in (2 * j, 2 * j + 1):
                        S.add("act", lambda e, c=c: e.activation(out=SQ[:, c, :], in_=src[:, c, :], func=AF.Square),
                              reads=[src_keys[c]], writes=[("SQ", c)], prio=0)
                    S.add("dve", lambda e, j=j: e.tensor_tensor(out=SQ[:, 2 * j, :], in0=SQ[:, 2 * j, :],
                                                                in1=SQ[:, 2 * j + 1, :], op=ALU.add),
                          reads=[("SQ", 2 * j), ("SQ", 2 * j + 1)], writes=[("SQ", 2 * j)], prio=0, dur=0.42)
                    pe_group(bank, [(PS[bank][:, :], ones[:, :], SQ[:, 2 * j, :], j == 0, j == 3)],
                             reads=[("SQ", 2 * j), ("const",)], prio=0)
            else:
                for c in range(n):
                    S.add("act", lambda e, c=c: e.activation(out=SQ[:, c, :], in_=src[:, c, :], func=AF.Square),
                          reads=[src_keys[c]], writes=[("SQ", c)], prio=0)
                    pe_group(bank, [(PS[bank][:, :], ones[:, :], SQ[:, c, :], c == 0, c == n - 1)],
                             reads=[("SQ", c), ("const",)], prio=0)
            slot = nxt("rs", 3)
            rstd_from_ms(RS[slot][:, :], PS[bank][:, :], [("ps", bank)], ("RS", slot), MT)
            return slot

        def normalize(out_fn, src_fn, gbase, n, slot, in_keys, out_keys, pool_chunks):
            for k in range(n):
                eng = "pool" if k in pool_chunks else "dve"
                S.add(eng, lambda e, k=k: e.scalar_tensor_tensor(
                    out=out_fn(k), in0=src_fn(k), scalar=pcol(gbase, k), in1=RS[slot][:, :],
                    op0=ALU.mult, op1=ALU.mult),
                    reads=[in_keys[k], ("RS", slot), ("const",)], writes=[out_keys[k]], prio=0)

        def stage0(m):
            xb, hm = m % 3, m % 2
            xk = [("X", xb, k) for k in range(8)]
            slot = rms_stats(X[xb], xk, 8, ONESD)
            normalize(lambda k: HT[:, hm, k, :], lambda k: X[xb][:, k, :], G1, 8, slot, xk,
                      [("HT", hm, k) for k in range(8)], ())

        def halo_stage():
            S.add("act", lambda e: e.activation(out=SQH, in_=XH, func=AF.Square),
                  reads=[("YB", 0)], writes=[("VH", 0)])
            bank = alloc_bank()
            pe_group(bank, [(PS[bank][:, 0:HW], ONESD[:, :], SQH[:, k, :], k == 0, k == 7) for k in range(8)],
                     reads=[("VH", 0), ("const",)])
            rstd_from_ms(RSH, PS[bank][:, 0:HW], [("ps", bank)], ("YB", 1), HW)
            for k in range(8):
                S.add("dve", lambda e, k=k: e.scalar_tensor_tensor(
                    out=HH[:, k, :], in0=XH[:, k, :], scalar=pcol(G1, k), in1=RSH,
                    op0=ALU.mult, op1=ALU.mult),
                    reads=[("YB", 0), ("YB", 1), ("const",)], writes=[("VH", 1)])
            bank2 = alloc_bank()
            WBC, WXU = RING[1], RING[2]
            mms = []
            for c in range(4):
                for k in range(8):
                    mms.append((PS[bank2][:, 2 * HW * c:2 * HW * c + HW],
                                WBC[:, k, 512 + c * 128:512 + (c + 1) * 128], HH[:, k, :], k == 0, k == 7))
                for k in range(8):
                    mms.append((PS[bank2][:, 2 * HW * c + HW:2 * HW * (c + 1)],
                                WXU[:, k, c * 128:(c + 1) * 128], HH[:, k, :], k == 0, k == 7))
            pe_group(bank2, mms, reads=[("VH", 1)] + wk(1, (2, 3)) + wk(2, (0, 1)))
            psv = PS[bank2][:, 0:8 * HW].rearrange("p (c j) -> p c j", c=4)
            S.add("act", lambda e: e.activation(out=CH, in_=psv[:, :, 0:HW], func=AF.Identity),
                  reads=[("ps", bank2)], writes=[("YB", 2)])
            S.add("dve", lambda e: e.tensor_tensor(out=ZH, in0=psv[:, :, HW:2 * HW], in1=CH,
                                                   op=ALU.mult),
                  reads=[("ps", bank2), ("YB", 2)], writes=[("YB", 3)])
            S.add("dve", lambda e: e.tensor_copy(out=Z[:, :, 0:2], in_=ZH[:, :, HW - 2:HW]),
                  reads=[("YB", 3)], writes=[("Z", c) for c in range(4)])

        def stage1_v(m):
            hm = m % 2
            WV = RING[0]
            banks = []
            for t in range(4):
                bank = alloc_bank()
                banks.append(bank)
                pe_group(bank, [(PS[bank][:, :], HT[:, hm, k, t * 128:(t + 1) * 128], WV[:, k, 0:512],
                                 k == 0, k == 7) for k in range(8)],
                         reads=[("HT", hm, k) for k in range(8)] + wk(0))
                r1, r2 = nxt("rl", 3), nxt("rl", 3)
                S.add("act", lambda e, t=t, bank=bank, r1=r1: e.activation(
                    out=RL[r1][:, :], in_=PS[bank][:, :], func=AF.Identity, accum_out=LNS[:, 0, t:t + 1]),
                    reads=[("ps", bank)], writes=[("RL", r1), ("LNS", 0, t)], prio=0)
                S.add("act", lambda e, t=t, bank=bank, r2=r2: e.activation(
                    out=RL[r2][:, :], in_=PS[bank][:, :], func=AF.Square, accum_out=LNS[:, 1, t:t + 1]),
                    reads=[("ps", bank)], writes=[("RL", r2), ("LNS", 1, t)], prio=0)
            skeys = [("LNS", a, t) for a in range(2) for t in range(4)]
            mkeys = [("LNM", t) for t in range(4)]
            S.add("dve", lambda e: e.tensor_scalar(out=LNM[:, :, 0], in0=LNS[:, 0, 0:4], scalar1=1.0 / 512.0,
                                                   scalar2=None, op0=ALU.mult),
                  reads=skeys, writes=mkeys, n=4, prio=0)
            S.add("dve", lambda e: e.tensor_tensor(out=LNS[:, 2, 0:4], in0=LNM[:, :, 0], in1=LNM[:, :, 0],
                                                   op=ALU.mult),
                  reads=mkeys, writes=[("LNS", 2)], n=4, prio=0)
            S.add("dve", lambda e: e.scalar_tensor_tensor(out=LNM[:, :, 1], in0=LNS[:, 1, 0:4], scalar=1.0 / 512.0,
                                                          in1=LNS[:, 2, 0:4], op0=ALU.mult, op1=ALU.subtract),
                  reads=skeys + [("LNS", 2)], writes=mkeys, n=4, prio=0)
            rstd_from_ms(LNR[:, :], LNM[:, :, 1], mkeys, ("LNR",), 4)
            S.add("dve", lambda e: e.scalar_tensor_tensor(out=LNB[:, :], in0=LNM[:, :, 0], scalar=-1.0,
                                                          in1=LNR[:, :], op0=ALU.mult, op1=ALU.mult),
                  reads=[("LNR",)] + [("LNM", t) for t in range(4)], writes=[("LNB",)])
            for t in range(4):
                bank = banks[t]
                S.add("act", lambda e, t=t, bank=bank: e.activation(
                    out=VH[:, t, :], in_=PS[bank][:, :], func=AF.Identity,
                    bias=LNB[:, t:t + 1], scale=LNR[:, t:t + 1]),
                    reads=[("ps", bank), ("LNR",), ("LNB",)], writes=[("VH", t)])

        def stage1_conv(m):
            hm = m % 2
            WBC, WXU = RING[1], RING[2]
            hreads = [("HT", hm, k) for k in range(8)]
            for c in range(4):
                bC, bX, bB = alloc_bank(), alloc_bank(), alloc_bank()
                pe_group(bC, [(PS[bC][:, :], WBC[:, k, 512 + c * 128:512 + (c + 1) * 128], HT[:, hm, k, :],
                               k == 0, k == 7) for k in range(8)], reads=hreads + wk(1, (2 + c // 2,)))
                pe_group(bX, [(PS[bX][:, :], WXU[:, k, c * 128:(c + 1) * 128], HT[:, hm, k, :],
                               k == 0, k == 7) for k in range(8)], reads=hreads + wk(2, (c // 2,)))
                pe_group(bB, [(PS[bB][:, :], WBC[:, k, c * 128:(c + 1) * 128], HT[:, hm, k, :],
                               k == 0, k == 7) for k in range(8)], reads=hreads + wk(1, (c // 2,)))
                ci = nxt("ct", 2)
                S.add("act", lambda e, ci=ci, bC=bC: e.activation(out=CT[ci][:, :], in_=PS[bC][:, :], func=AF.Identity),
                      reads=[("ps", bC)], writes=[("CT", ci)])
                S.add("dve", lambda e, c=c, ci=ci, bX=bX: e.tensor_tensor(
                    out=Z[:, c, 2:2 + MT], in0=PS[bX][:, :], in1=CT[ci][:, :], op=ALU.mult),
                    reads=[("ps", bX), ("CT", ci)], writes=[("Z", c)])
                S.add("act", lambda e, c=c: e.activation(out=YA[:, c, :], in_=Z[:, c, 2:2 + MT], func=AF.Identity,
                                                         scale=pcol(CW, 3 * c + 2)),
                      reads=[("Z", c), ("const",)], writes=[("YA", c)])
                S.add("dve", lambda e, c=c: e.scalar_tensor_tensor(
                    out=YA[:, c, :], in0=Z[:, c, 1:1 + MT], scalar=pcol(CW, 3 * c + 1), in1=YA[:, c, :],
                    op0=ALU.mult, op1=ALU.add),
                    reads=[("Z", c), ("YA", c), ("const",)], writes=[("YA", c)])
                S.add("dve", lambda e, c=c: e.scalar_tensor_tensor(
                    out=YA[:, c, :], in0=Z[:, c, 0:MT], scalar=pcol(CW, 3 * c + 0), in1=YA[:, c, :],
                    op0=ALU.mult, op1=ALU.add),
                    reads=[("Z", c), ("YA", c), ("const",)], writes=[("YA", c)])
                S.add("dve", lambda e, c=c, bB=bB: e.tensor_tensor(
                    out=YA[:, c, :], in0=PS[bB][:, :], in1=YA[:, c, :], op=ALU.mult),
                    reads=[("ps", bB), ("YA", c)], writes=[("YA", c)])
                S.add("pool", lambda e, c=c: e.tensor_copy(out=Z[:, c, 0:2], in_=Z[:, c, MT:MT + 2]),
                      reads=[("Z", c)], writes=[("Z", c)], dur=0.3)

        def stage1_su(m):
            hm = m % 2
            WXU = RING[2]
            hreads = [("HT", hm, k) for k in range(8)]
            for c in range(4):
                bS, bU = alloc_bank(), alloc_bank()
                mms = []
                for t in range(4):
                    for hh in range(2):
                        h = 2 * c + hh
                        mms.append((PS[bS][64 * hh:64 * hh + 64, t * 128:(t + 1) * 128],
                                    VH[:, t, h * 64:(h + 1) * 64], WST[:, h, :], True, True))
                pe_group(bS, mms, reads=[("VH", t) for t in range(4)] + [("WST",)])
                pe_group(bU, [(PS[bU][:, :], WXU[:, k, 512 + c * 128:512 + (c + 1) * 128], HT[:, hm, k, :],
                               k == 0, k == 7) for k in range(8)], reads=hreads + wk(2, (2 + c // 2,)))
                ti = nxt("t2", 2)
                bias_b = BIAS[:, c, :].unsqueeze(1).broadcast_to([128, 4, 128])
                S.add("dve", lambda e, c=c, ti=ti, bS=bS, bias_b=bias_b: e.scalar_tensor_tensor(
                    out=T2[ti][:, :].rearrange("p (t i) -> p t i", t=4),
                    in0=PS[bS][:, :].rearrange("p (t i) -> p t i", t=4),
                    scalar=pcol(GLN, c), in1=bias_b, op0=ALU.mult, op1=ALU.add),
                    reads=[("ps", bS), ("BIAS",), ("const",)], writes=[("T2", ti)])
                S.add("dve", lambda e, c=c, ti=ti, bU=bU: e.tensor_tensor(
                    out=YB[:, c, :], in0=PS[bU][:, :], in1=T2[ti][:, :], op=ALU.mult),
                    reads=[("ps", bU), ("T2", ti)], writes=[("YB", c)])

        def y_norm(m, which):
            mm_ = m % 2
            if which == "a":
                src, key, gbase, ebase = YA, "YA", GCONV, 0
            else:
                src, key, gbase, ebase = YB, "YB", GGM, 4
            yk = [(key, c) for c in range(4)]
            slot = rms_stats(src, yk, 4, ONESE)
            normalize(lambda c: YN[:, mm_, ebase + c, :], lambda c: src[:, c, :], gbase, 4, slot, yk,
                      [("ATY", mm_ * 8 + ebase + c) for c in range(4)], ())

        def out_proj(m):
            xb, mm_ = m % 3, m % 2
            WO = RING[3]
            for dc in range(8):
                bank = alloc_bank()
                pe_group(bank, [(PS[bank][:, :], WO[:, e_, dc * 128:(dc + 1) * 128], YN[:, mm_, e_, :],
                                 e_ == 0, e_ == 7) for e_ in range(8)],
                         reads=[("ATY", mm_ * 8 + e_) for e_ in range(8)] + wk(3))
                S.add("dve", lambda e, dc=dc, bank=bank: e.tensor_tensor(
                    out=X[xb][:, dc, :], in0=PS[bank][:, :], in1=X[xb][:, dc, :], op=ALU.add),
                    reads=[("ps", bank), ("X", xb, dc)], writes=[("X", xb, dc)])

        def norm2(m):
            xb, hm = m % 3, m % 2
            xk = [("X", xb, k) for k in range(8)]
            slot = rms_stats(X[xb], xk, 8, ONESD)
            normalize(lambda k: HT[:, hm, k, :], lambda k: X[xb][:, k, :], G2, 8, slot, xk,
                      [("HT", hm, k) for k in range(8)], ())

        def ffn_up(m, p):
            hm = m % 2
            slot = (4 + 2 * p) % 4
            WU = RING[slot]
            hreads = [("HT", hm, k) for k in range(8)]
            for fc in range(8):
                bank = alloc_bank()
                pe_group(bank, [(PS[bank][:, :], WU[:, k, fc * 128:(fc + 1) * 128], HT[:, hm, k, :],
                                 k == 0, k == 7) for k in range(8)], reads=hreads + wk(slot))
                ri = nxt("rl", 3)
                S.add("act", lambda e, ri=ri, bank=bank: e.activation(out=RL[ri][:, :], in_=PS[bank][:, :],
                                                                      func=AF.Relu),
                      reads=[("ps", bank)], writes=[("RL", ri)])
                S.add("act", lambda e, ri=ri, fc=fc: e.activation(
                    out=AT[:, fc, hm * MT:(hm + 1) * MT], in_=RL[ri][:, :], func=AF.Square),
                    reads=[("RL", ri)], writes=[("ATY", fc * 2 + hm)])

        def ffn_down(m, p):
            xb, hm = m % 3, m % 2
            slot = (5 + 2 * p) % 4
            WD = RING[slot]
            for dc in range(8):
                bank = alloc_bank()
                pe_group(bank, [(PS[bank][:, :], WD[:, fc, dc * 128:(dc + 1) * 128],
                                 AT[:, fc, hm * MT:(hm + 1) * MT], fc == 0, fc == 7) for fc in range(8)],
                         reads=[("ATY", fc * 2 + hm) for fc in range(8)] + wk(slot))
                S.add("dve", lambda e, dc=dc, bank=bank: e.tensor_tensor(
                    out=X[xb][:, dc, :], in0=PS[bank][:, :], in1=X[xb][:, dc, :], op=ALU.add),
                    reads=[("ps", bank), ("X", xb, dc)], writes=[("X", xb, dc)])

        def final(m):
            xb = m % 3
            xk = [("X", xb, k) for k in range(8)]
            slot = rms_stats(X[xb], xk, 8, ONESD)
            normalize(lambda k: X[xb][:, k, :], lambda k: X[xb][:, k, :], GF, 8, slot, xk, xk, ())
            for j in range(4):
                store_x(m, j)

        wv = 3.0 + 2.1e6 / 160e3
        tokv = [("tok", 9)]
        cast_piece_to_ring(0, dur=wv, tok=tokv[0])
        w0 = 3.0 + 3.2e6 / 200e3
        toks = [("tok", j) for j in range(3)]
        for j, (i, q) in enumerate(((2, 0), (1, 2), (1, 0))):
            cast_quarter(i, q, dur=w0, reads=tokv, tok=toks[j])
        w1 = 3.0 + 3.2e6 / 200e3
        toks1 = [("tok", 4 + j) for j in range(3)]
        for j, (i, q) in enumerate(((2, 1), (1, 3), (1, 1))):
            cast_quarter(i, q, dur=w1, reads=toks, tok=toks1[j])
        w2 = 3.0 + 6.3e6 / 220e3
        for (i, q) in ((2, 2), (2, 3)):
            cast_quarter(i, q, dur=w2, reads=toks1)
        save_piece(1)
        save_piece(2)
        cast_piece_to_ring(3, dur=w2, reads=toks1)
        S.add("sp", lambda e: e.dma_start(out=PRM[:, :], in_=prm_d), writes=[("prm",)], dma_key=("misc", 0), prio=0)
        S.add("sp", lambda e: e.dma_start(out=BSF[:, :, :], in_=bsf_d), writes=[("bsf",)], dma_key=("misc", 1), prio=0)
        WSF = [RL[0][:, :].rearrange("p (h i) -> p h i", h=4), RL[1][:, :].rearrange("p (h i) -> p h i", h=4)]
        S.add("sp", lambda e: e.dma_start(out=WSF[0], in_=wst_d[:, 0:4, :]), writes=[("RL", 0)], dma_key=("misc", 2), prio=0)
        S.add("sp", lambda e: e.dma_start(out=WSF[1], in_=wst_d[:, 4:8, :]), writes=[("RL", 1)], dma_key=("misc", 3), prio=0)
        S.add("sp", lambda e: e.dma_start(out=XH, in_=xh), writes=[("YB", 0)], dma_key=("misc", 4), prio=0)
        load_x(0)
        load_x(1, reads=toks)

        def consts(e):
            e.memset(ONESD[:, :], 1.0 / D)
            e.memset(ONESE[:, :], 1.0 / 512.0)
            e.memset(ONE1[:, :], 1.0)
            return e.memset(EPST[:, :], EPS)
        S.add("dve", consts, writes=[("const0",)])
        S.add("dve", lambda e: e.tensor_copy(out=WST[:, 0:4, :], in_=WSF[0]),
              reads=[("RL", 0), ("const0",), ("prm",)], writes=[("WST",), ("const",)])
        S.add("dve", lambda e: e.tensor_copy(out=WST[:, 4:8, :], in_=WSF[1]),
              reads=[("RL", 1)], writes=[("WST",)])
        S.add("dve", lambda e: e.memset(WST[64:128, :, 0:64], 0.0), reads=[("WST",)], writes=[("WST",)])
        bank = alloc_bank()
        mms = []
        for c in range(4):
            for hh in range(2):
                mms.append((PS[bank][64 * hh:64 * hh + 64, c * 128:(c + 1) * 128], ONE1[:, :],
                            WST[:, 2 * c + hh, :], True, True))
        pe_group(bank, mms, reads=[("WST",), ("const",)])
        S.add("dve", lambda e: e.tensor_tensor(out=BIAS[:, :, :],
                                               in0=PS[bank][:, :].rearrange("p (c i) -> p c i", c=4),
                                               in1=PRM[:, BLN:BLN + 4].unsqueeze(2).broadcast_to([128, 4, 128]),
                                               op=ALU.mult),
              reads=[("ps", bank), ("const",)], writes=[("BIAS",)])
        S.add("dve", lambda e: e.tensor_tensor(out=BIAS[:, :, :], in0=BIAS[:, :, :], in1=BSF[:, :, :], op=ALU.add),
              reads=[("BIAS",), ("bsf",)], writes=[("BIAS",)])

        stage0(0)
        halo_stage()
        if debug:
            S.add("sp", lambda e: e.dma_start(out=dbg_z, in_=Z[:, :, 0:2]), reads=[("Z", c) for c in range(4)],
                  writes=[("dbg", 0)], dma_key=("dbg", 0))
            S.add("sp", lambda e: e.dma_start(out=dbg_ch, in_=CH), reads=[("YB", 2)],
                  writes=[("dbg", 1)], dma_key=("dbg", 1))
            S.add("sp", lambda e: e.dma_start(out=dbg_rsh, in_=RSH), reads=[("YB", 1)],
                  writes=[("dbg", 2)], dma_key=("dbg", 2))
            S.add("sp", lambda e: e.dma_start(out=dbg_hh, in_=HH), reads=[("VH", 1)],
                  writes=[("dbg", 3)], dma_key=("dbg", 3))

        for g in range(NGRP):
            mA, mB = 2 * g, 2 * g + 1
            last = g == NGRP - 1
            stage1_v(mA)
            stage1_conv(mA)
            stage1_su(mA)
            stage0(mB)
            stage1_v(mB)
            load_piece(4, g)
            y_norm(mA, "a")
            stage1_conv(mB)
            load_piece(5, g)
            y_norm(mA, "b")
            stage1_su(mB)
            load_piece(6, g)
            out_proj(mA)
            y_norm(mB, "a")
            y_norm(mB, "b")
            norm2(mA)
            out_proj(mB)
            load_piece(7, g)
            if not last:
                load_x(mA + 2)
            norm2(mB)
            for p in range(4):
                ffn_up(mA, p)
                ffn_up(mB, p)
                idx = 8 + 2 * p
                if idx < NPIECE:
                    load_piece(idx, g)
                elif not last:
                    load_piece(idx - NPIECE)
                if p == 3 and not last:
                    stage0(mA + 2)
                ffn_down(mA, p)
                ffn_down(mB, p)
                idx = 9 + 2 * p
                if idx < NPIECE:
                    load_piece(idx, g)
                elif not last:
                    load_piece(idx - NPIECE)
            final(mA)
            final(mB)
            if not last:
                load_x(mB + 2)

        S.schedule()
        dma_keys = S.resolve()
        eng_sems = {e_: es.enter_context(nc.semaphore(f"sem_{e_}")) for e_ in ENGS}
        dma_sems = {k: es.enter_context(nc.semaphore("dma_" + "_".join(str(x) for x in k))) for k in dma_keys}
        S.assign(eng_sems, dma_sems)

        block = es.enter_context(nc.Block())

        @block.tensor
        def _(e):
            S.emit("pe", e)

        @block.scalar
        def _(e):
            S.emit("act", e)

        @block.vector
        def _(e):
            S.emit("dve", e)

        @block.gpsimd
        def _(e):
            S.emit("pool", e)

        @block.sync
        def _(e):
            S.emit("sp", e)
            for k in dma_sems:
                if k[0] == "xs":
                    e.wait_ge(dma_sems[k], S.final_dma[k])
    return nc


_CACHE = {}


def _host_inputs(inp):
    f = lambda a: np.ascontiguousarray(np.asarray(a, dtype=np.float32))
    x = f(inp["x"])
    prm = np.zeros((128, NPRM), np.float32)

    def cols(v, n):
        return f(v).reshape(n, 128).T

    prm[:, 0:8] = cols(inp["norm1_g"][0], 8)
    prm[:, 8:16] = cols(inp["norm2_g"][0], 8)
    prm[:, 16:24] = cols(inp["final_g"], 8)
    prm[:, 24:28] = cols(inp["out_norm_conv_g"][0], 4)
    prm[:, 28:32] = cols(inp["out_norm_gmlp_g"][0], 4)
    prm[:, 32:36] = cols(inp["gmlp_ln_g"][0], 4)
    prm[:, 36:40] = cols(inp["gmlp_ln_b"][0], 4)
    cw = f(inp["conv_w"][0]).reshape(4, 128, 3).transpose(1, 0, 2).reshape(128, 12)
    prm[:, 40:52] = cw
    bs = f(inp["gmlp_bs"][0])
    bsf = np.repeat(bs.reshape(4, 2, 1, 128), 64, axis=2).reshape(4, 128, 128).transpose(1, 0, 2)
    wst = f(inp["gmlp_ws"][0]).transpose(2, 0, 1)
    shared = {
        "prm": np.ascontiguousarray(prm),
        "bsf": np.ascontiguousarray(bsf),
        "wst": np.ascontiguousarray(wst),
        "w_in": f(inp["w_in"][0]),
        "w_out": f(inp["w_out"][0]),
        "w_up": f(inp["w_up"][0]),
        "w_down": f(inp["w_down"][0]),
    }
    maps = []
    for c in range(NCORES):
        b, half = c // 2, c % 2
        t0 = half * TOK
        xc = x[b, t0:t0 + TOK, :]
        xh = np.zeros((HW, D), np.float32)
        if half == 1:
            xh[HW - 2:HW] = x[b, t0 - 2:t0, :]
        m = dict(shared)
        m["xT"] = np.ascontiguousarray(xc.reshape(NMT, MT, 8, 128).transpose(0, 3, 2, 1)).reshape(NMT, 128, 8 * MT)
        m["xh"] = np.ascontiguousarray(xh.T.reshape(8, 128, HW).transpose(1, 0, 2))
        maps.append(m)
    return maps


def kernel(**inputs):
    if "nc" not in _CACHE:
        _CACHE["nc"] = build_program()
    nc = _CACHE["nc"]
    maps = _host_inputs(inputs)
    res = run_bass_kernel_spmd(nc, maps, core_ids=list(range(NCORES)))
    out = np.empty((BATCH, SEQ, D), np.float32)
    for c in range(NCORES):
        b, half = c // 2, c % 2
        o = np.asarray(res.results[c]["oT"]).reshape(NMT, 128, 8, MT)
        out[b, half * TOK:(half + 1) * TOK, :] = o.transpose(0, 3, 2, 1).reshape(TOK, D)
    return out
```
